# Optimizing a Trainium2 kernel written in Bass

```python
import jax
import jax.numpy as jnp
from jax import lax
import numpy as np

D_MODEL = 2048
BATCH = 4
SEQ = 8192
DEPTH = 2

GRID_W = 64
CTX_LEN = 256
N_EVEN = (DEPTH + 1) // 2
N_ODD = DEPTH // 2
EPS = 1e-6
D_FF = 5632
FFN_STEP = 0.5
LRU_WIDTH = D_MODEL // 2
LRU_BLOCKS = 8
LRU_BLOCK = LRU_WIDTH // LRU_BLOCKS
LRU_C = 8.0
CONV_W = 4
CONV_LEFT = 2
RET_HEADS = 4
RET_DK = 256
RET_DV = 256
RET_QK = RET_HEADS * RET_DK
RET_V = RET_HEADS * RET_DV
RET_CHUNK = 128
RET_THETA = 10000.0
EV_IN = 2 * LRU_WIDTH + 2 * RET_QK + 2 * RET_V
EV_MIX = LRU_WIDTH + RET_V
POOL_WINDOWS = (2, 4, 8, 16)
POOL_GROUP = 128
POOL_WIDTH = POOL_GROUP * len(POOL_WINDOWS)
ATT_HEADS = 12
KV_HEADS = 4
GROUP = ATT_HEADS // KV_HEADS
HEAD_DIM = 128
ATT_QW = ATT_HEADS * HEAD_DIM
ATT_KVW = KV_HEADS * HEAD_DIM
Q_BLOCK = 128
ROPE_THETA = 10000.0
OD_IN = POOL_WIDTH + ATT_QW + 2 * ATT_KVW
OD_MIX = POOL_WIDTH + ATT_QW

kernel_name = "hybrid_lru_retention_pool_gqa_diffusion_block"


def rmsnorm(x, g):
    xf = x.astype(jnp.float32)
    y = xf * lax.rsqrt(jnp.mean(xf * xf, axis=-1, keepdims=True) + EPS)
    return (y * g.astype(jnp.float32)).astype(x.dtype)


def modulate(x, g, shift, scale):
    return rmsnorm(x, g) * (1 + scale) + shift


def residual(x, y, g_post, gate, w):
    return x + w * gate * rmsnorm(y, g_post)


def swiglu(h, w_gate, w_up, w_down):
    return (jax.nn.silu(h @ w_gate) * (h @ w_up)) @ w_down


def split_heads(t, h, d):
    return t.reshape(t.shape[0], t.shape[1], h, d)


def apply_rotary(x, cos, sin):
    x1, x2 = jnp.split(x, 2, axis=-1)
    c = cos[:, None, :]
    s = sin[:, None, :]
    return jnp.concatenate([x1 * c - x2 * s, x1 * s + x2 * c], axis=-1).astype(x.dtype)


def dwconv(x, w, b):
    L = x.shape[1]
    xp = jnp.pad(x, ((0, 0), (CONV_LEFT, CONV_W - 1 - CONV_LEFT), (0, 0)))
    y = xp[:, 0:L] * w[0]
    for k in range(1, CONV_W):
        y = y + xp[:, k:k + L] * w[k]
    return y + b


def blockdiag(x, w, b):
    B_, L, _ = x.shape
    xb = x.reshape(B_, L, LRU_BLOCKS, LRU_BLOCK)
    return jnp.einsum('blnc,ncd->blnd', xb, w).reshape(B_, L, LRU_WIDTH) + b


def linear_scan(a, b, h0):
    def comb(l, r):
        return r[0] * l[0], r[0] * l[1] + r[1]
    a_cum, b_cum = lax.associative_scan(comb, (a, b), axis=1)
    if h0 is None:
        return b_cum
    return b_cum + a_cum * h0[:, None, :]


def rglru_coeffs(u, wa, ba, wx, bx, lam):
    r = jax.nn.sigmoid(blockdiag(u, wa, ba))
    i = jax.nn.sigmoid(blockdiag(u, wx, bx))
    log_a = -LRU_C * r * jax.nn.softplus(-lam.astype(jnp.float32))
    a = jnp.exp(log_a)
    bterm = jnp.sqrt(-jnp.expm1(2.0 * log_a)) * (i * u)
    return a, bterm


def rglru_bidir(ul, uc, wa, ba, wx, bx, lam):
    outs_l, outs_c = [], []
    for d in range(2):
        al, bl = rglru_coeffs(ul, wa[d], ba[d], wx[d], bx[d], lam[d])
        ac, bc = rglru_coeffs(uc, wa[d], ba[d], wx[d], bx[d], lam[d])
        if d == 1:
            al, bl, ac, bc = [jnp.flip(t, axis=1) for t in (al, bl, ac, bc)]
        h_c = linear_scan(ac, bc, None)
        h_l = linear_scan(al, bl, h_c[:, -1])
        if d == 1:
            h_c = jnp.flip(h_c, axis=1)
            h_l = jnp.flip(h_l, axis=1)
        outs_l.append(h_l)
        outs_c.append(h_c)
    return outs_l[0] + outs_l[1], outs_c[0] + outs_c[1]


def retention_scan(q, k, v, log_g, s0):
    B_, H, L, _ = q.shape
    C = RET_CHUNK
    n = L // C
    idx = jnp.arange(C, dtype=jnp.float32)
    diff = idx[:, None] - idx[None, :]
    lg = log_g[:, None, None]
    intra = jnp.where(diff >= 0, jnp.exp(lg * jnp.maximum(diff, 0.0)), 0.0)
    q_dec = jnp.exp(log_g[:, None] * (idx + 1.0))
    k_dec = jnp.exp(log_g[:, None] * (C - 1.0 - idx))
    s_dec = jnp.exp(log_g * C)

    def to_chunks(t):
        return t.reshape(B_, H, n, C, t.shape[-1]).transpose(2, 0, 1, 3, 4)

    def step(s, qkv):
        qc, kc, vc = qkv
        scores = jnp.einsum('bhid,bhjd->bhij', qc, kc) * intra
        o = (jnp.einsum('bhij,bhje->bhie', scores, vc)
             + jnp.einsum('bhid,bhde->bhie', qc * q_dec[..., None], s))
        s_new = s * s_dec[:, None, None] + jnp.einsum('bhjd,bhje->bhde', kc * k_dec[..., None], vc)
        return s_new, o

    s_fin, o = lax.scan(step, s0, (to_chunks(q), to_chunks(k), to_chunks(v)))
    o = o.transpose(1, 2, 0, 3, 4).reshape(B_, H, L, o.shape[-1])
    return o, s_fin


def retention_bidir(ql, kl, vl, qc, kc, vc, log_g):
    B_, H = ql.shape[0], ql.shape[1]
    s0 = jnp.zeros((B_, H, RET_DK, RET_DV), jnp.float32)
    outs_l, outs_c = [], []
    for d in range(2):
        seqs = (ql, kl, vl, qc, kc, vc)
        if d == 1:
            seqs = tuple(jnp.flip(t, axis=2) for t in seqs)
        o_c, s_c = retention_scan(seqs[3], seqs[4], seqs[5], log_g[d], s0)
        o_l, _ = retention_scan(seqs[0], seqs[1], seqs[2], log_g[d], s_c)
        if d == 1:
            o_c = jnp.flip(o_c, axis=2)
            o_l = jnp.flip(o_l, axis=2)
        outs_l.append(o_l)
        outs_c.append(o_c)
    return outs_l[0] + outs_l[1], outs_c[0] + outs_c[1]


def head_groupnorm(o, g):
    mu = jnp.mean(o, axis=-1, keepdims=True)
    var = jnp.mean(jnp.square(o - mu), axis=-1, keepdims=True)
    y = (o - mu) * lax.rsqrt(var + EPS)
    B_, H, L, dv = y.shape
    return y.transpose(0, 2, 1, 3).reshape(B_, L, H * dv) * g


def to_bhld(t):
    return t.astype(jnp.float32).transpose(0, 2, 1, 3)


def even_mixer(hl, hc, w_in, w_out, conv_w, conv_b, wa, ba, wx, bx, lam, decay_logit, gn_g, cos, sin):
    splits = [LRU_WIDTH, 2 * LRU_WIDTH, 2 * LRU_WIDTH + RET_QK, 2 * LRU_WIDTH + 2 * RET_QK,
              2 * LRU_WIDTH + 2 * RET_QK + RET_V]
    gl, rl, ql, kl, vl, ol = jnp.split(hl @ w_in, splits, axis=-1)
    gc, rc, qc, kc, vc, oc = jnp.split(hc @ w_in, splits, axis=-1)
    ul = dwconv(rl, conv_w, conv_b).astype(jnp.float32)
    uc = dwconv(rc, conv_w, conv_b).astype(jnp.float32)
    hl_lru, hc_lru = rglru_bidir(ul, uc, wa, ba, wx, bx, lam)
    lru_l = jax.nn.gelu(gl.astype(jnp.float32)) * hl_lru
    lru_c = jax.nn.gelu(gc.astype(jnp.float32)) * hc_lru
    k_scale = RET_DK ** -0.5
    ql = to_bhld(apply_rotary(split_heads(ql, RET_HEADS, RET_DK), cos, sin))
    kl = to_bhld(apply_rotary(split_heads(kl, RET_HEADS, RET_DK), cos, sin)) * k_scale
    vl = to_bhld(split_heads(vl, RET_HEADS, RET_DV))
    qc = to_bhld(split_heads(qc, RET_HEADS, RET_DK))
    kc = to_bhld(split_heads(kc, RET_HEADS, RET_DK)) * k_scale
    vc = to_bhld(split_heads(vc, RET_HEADS, RET_DV))
    log_g = -jax.nn.softplus(-decay_logit.astype(jnp.float32))
    rl_out, rc_out = retention_bidir(ql, kl, vl, qc, kc, vc, log_g)
    ret_l = head_groupnorm(rl_out, gn_g) * jax.nn.silu(ol.astype(jnp.float32))
    ret_c = head_groupnorm(rc_out, gn_g) * jax.nn.silu(oc.astype(jnp.float32))
    yl = jnp.concatenate([lru_l, ret_l], axis=-1).astype(hl.dtype) @ w_out
    yc = jnp.concatenate([lru_c, ret_c], axis=-1).astype(hc.dtype) @ w_out
    return yl, yc


def multiscale_pool(x, pool_w, pool_scale):
    B_, L, _ = x.shape
    xf = x.astype(jnp.float32)
    cs = jnp.concatenate([jnp.zeros((B_, 1, POOL_WIDTH), jnp.float32), jnp.cumsum(xf, axis=1)], axis=1)
    t = jnp.arange(L)
    outs = []
    for gi, w in enumerate(POOL_WINDOWS):
        lo = jnp.clip(t - w // 2, 0, L)
        hi = jnp.clip(t + w // 2, 0, L)
        sl = slice(gi * POOL_GROUP, (gi + 1) * POOL_GROUP)
        csg = cs[..., sl]
        cnt = (hi - lo).astype(jnp.float32)[None, :, None]
        mean = (csg[:, hi] - csg[:, lo]) / cnt
        outs.append(jnp.einsum('blc,cd->bld', mean - xf[..., sl], pool_w[gi].astype(jnp.float32)))
    return (jnp.concatenate(outs, axis=-1) * pool_scale).astype(x.dtype)


def attend(q, k, v):
    B_, Lq = q.shape[0], q.shape[1]
    nb = Lq // Q_BLOCK
    qb = q.reshape(B_, nb, Q_BLOCK, KV_HEADS, GROUP, HEAD_DIM).transpose(1, 0, 3, 4, 2, 5)
    kt = k.transpose(0, 2, 1, 3)
    vt = v.transpose(0, 2, 1, 3)
    scale = HEAD_DIM ** -0.5

    def blk(qi):
        s = jnp.einsum('bkgqd,bksd->bkgqs', qi, kt, preferred_element_type=jnp.float32) * scale
        p = jax.nn.softmax(s, axis=-1)
        return jnp.einsum('bkgqs,bksd->bkgqd', p.astype(vt.dtype), vt)

    o = lax.map(blk, qb)
    return o.transpose(1, 0, 4, 2, 3, 5).reshape(B_, Lq, ATT_QW)


def odd_mixer(hl, hc, w_in, w_out, pool_w, pool_scale, q_g, k_g, cos, sin, with_ctx):
    splits = [POOL_WIDTH, POOL_WIDTH + ATT_QW, POOL_WIDTH + ATT_QW + ATT_KVW]
    pool_l, ql, kl, vl = jnp.split(hl @ w_in, splits, axis=-1)
    ql = apply_rotary(rmsnorm(split_heads(ql, ATT_HEADS, HEAD_DIM), q_g), cos, sin)
    kl = apply_rotary(rmsnorm(split_heads(kl, KV_HEADS, HEAD_DIM), k_g), cos, sin)
    vl = split_heads(vl, KV_HEADS, HEAD_DIM)
    if with_ctx:
        pool_c, qc, kc, vc = jnp.split(hc @ w_in, splits, axis=-1)
    else:
        kc, vc = jnp.split(hc @ w_in[:, POOL_WIDTH + ATT_QW:], [ATT_KVW], axis=-1)
    kc = rmsnorm(split_heads(kc, KV_HEADS, HEAD_DIM), k_g)
    vc = split_heads(vc, KV_HEADS, HEAD_DIM)
    att_l = attend(ql, jnp.concatenate([kc, kl], axis=1), jnp.concatenate([vc, vl], axis=1))
    yl = jnp.concatenate([multiscale_pool(pool_l, pool_w, pool_scale), att_l.astype(hl.dtype)], axis=-1) @ w_out
    if not with_ctx:
        return yl, None
    qc = rmsnorm(split_heads(qc, ATT_HEADS, HEAD_DIM), q_g)
    att_c = attend(qc, kc, vc)
    yc = jnp.concatenate([multiscale_pool(pool_c, pool_w, pool_scale), att_c.astype(hc.dtype)], axis=-1) @ w_out
    return yl, yc


def setup_inputs(seed: int = 0) -> dict:
    key = jax.random.key(seed)
    ks = list(jax.random.split(key, 28))

    def nrm(i, shape, s):
        return jax.random.normal(ks[i], shape, jnp.float32) * s

    D = D_MODEL
    u = jax.random.uniform(ks[19], (N_EVEN, 2, LRU_WIDTH), jnp.float32, 0.9, 0.999)
    a = u ** (1.0 / LRU_C)
    lam = jnp.log(a) - jnp.log1p(-a)
    gam = 1.0 - 2.0 ** (-5.0 - jnp.arange(RET_HEADS, dtype=jnp.float32))
    decay_base = jnp.log(gam) - jnp.log1p(-gam)
    return {
        "x": nrm(0, (BATCH, SEQ, D), 1.0),
        "c": nrm(1, (BATCH, D), 1.0),
        "ctx": nrm(2, (BATCH, CTX_LEN, D), 1.0),
        "c_ctx": nrm(3, (D,), 1.0),
        "mod_w": nrm(4, (DEPTH, D, 9 * D), 0.5 * D ** -0.5),
        "mod_b": nrm(5, (DEPTH, 9 * D), 0.02),
        "norm_pre": 1.0 + nrm(6, (DEPTH, 3, D), 0.02),
        "norm_post": 1.0 + nrm(7, (DEPTH, 3, D), 0.02),
        "ffn_gate": nrm(8, (DEPTH, 2, D, D_FF), D ** -0.5),
        "ffn_up": nrm(9, (DEPTH, 2, D, D_FF), D ** -0.5),
        "ffn_down": nrm(10, (DEPTH, 2, D_FF, D), D_FF ** -0.5),
        "ev_w_in": nrm(11, (N_EVEN, D, EV_IN), D ** -0.5),
        "ev_w_out": nrm(12, (N_EVEN, EV_MIX, D), EV_MIX ** -0.5),
        "lru_conv_w": nrm(13, (N_EVEN, CONV_W, LRU_WIDTH), CONV_W ** -0.5),
        "lru_conv_b": nrm(14, (N_EVEN, LRU_WIDTH), 0.01),
        "lru_wa": nrm(15, (N_EVEN, 2, LRU_BLOCKS, LRU_BLOCK, LRU_BLOCK), LRU_BLOCK ** -0.5),
        "lru_ba": nrm(16, (N_EVEN, 2, LRU_WIDTH), 0.01),
        "lru_wx": nrm(17, (N_EVEN, 2, LRU_BLOCKS, LRU_BLOCK, LRU_BLOCK), LRU_BLOCK ** -0.5),
        "lru_bx": nrm(18, (N_EVEN, 2, LRU_WIDTH), 0.01),
        "lru_lambda": lam,
        "ret_decay_logit": decay_base + nrm(20, (N_EVEN, 2, RET_HEADS), 0.05),
        "ret_gn": 1.0 + nrm(21, (N_EVEN, RET_V), 0.02),
        "od_w_in": nrm(22, (N_ODD, D, OD_IN), D ** -0.5),
        "od_w_out": nrm(23, (N_ODD, OD_MIX, D), OD_MIX ** -0.5),
        "pool_w": nrm(24, (N_ODD, len(POOL_WINDOWS), POOL_GROUP, POOL_GROUP), POOL_GROUP ** -0.5),
        "pool_scale": 1.0 + nrm(25, (N_ODD, POOL_WIDTH), 0.1),
        "q_norm": 1.0 + nrm(26, (N_ODD, HEAD_DIM), 0.02),
        "k_norm": 1.0 + nrm(27, (N_ODD, HEAD_DIM), 0.02),
    }


def reference(x, c, ctx, c_ctx, mod_w, mod_b, norm_pre, norm_post, ffn_gate, ffn_up, ffn_down,
              ev_w_in, ev_w_out, lru_conv_w, lru_conv_b, lru_wa, lru_ba, lru_wx, lru_bx, lru_lambda,
              ret_decay_logit, ret_gn, od_w_in, od_w_out, pool_w, pool_scale, q_norm, k_norm):
    D = D_MODEL
    S = x.shape[1]
    rows = S // GRID_W
    row = jnp.repeat(jnp.arange(rows, dtype=jnp.float32), GRID_W)
    col = jnp.tile(jnp.arange(GRID_W, dtype=jnp.float32), rows)
    n_ax = HEAD_DIM // 4
    f_ax = ROPE_THETA ** (-jnp.arange(n_ax, dtype=jnp.float32) / n_ax)
    ang2 = jnp.concatenate([row[:, None] * f_ax, col[:, None] * f_ax], axis=-1)
    cos2, sin2 = jnp.cos(ang2), jnp.sin(ang2)
    n_r = RET_DK // 2
    f_r = RET_THETA ** (-jnp.arange(n_r, dtype=jnp.float32) / n_r)
    ang1 = jnp.arange(S, dtype=jnp.float32)[:, None] * f_r
    cos1, sin1 = jnp.cos(ang1), jnp.sin(ang1)

    sc = jax.nn.silu(c)
    scc = jax.nn.silu(c_ctx)
    xl, xc = x, ctx
    for li in range(DEPTH):
        last = li == DEPTH - 1
        n_ctx_sub = 2 if last else 3
        mod_l = (sc @ mod_w[li] + mod_b[li]).reshape(-1, 3, 3, 1, D)
        mod_c = (scc @ mod_w[li][:, :n_ctx_sub * 3 * D] + mod_b[li, :n_ctx_sub * 3 * D]).reshape(n_ctx_sub, 3, D)

        hl = modulate(xl, norm_pre[li, 0], mod_l[:, 0, 0], mod_l[:, 0, 1])
        hc = modulate(xc, norm_pre[li, 0], mod_c[0, 0], mod_c[0, 1])
        xl = residual(xl, swiglu(hl, ffn_gate[li, 0], ffn_up[li, 0], ffn_down[li, 0]), norm_post[li, 0], mod_l[:, 0, 2], FFN_STEP)
        xc = residual(xc, swiglu(hc, ffn_gate[li, 0], ffn_up[li, 0], ffn_down[li, 0]), norm_post[li, 0], mod_c[0, 2], FFN_STEP)

        hl = modulate(xl, norm_pre[li, 1], mod_l[:, 1, 0], mod_l[:, 1, 1])
        hc = modulate(xc, norm_pre[li, 1], mod_c[1, 0], mod_c[1, 1])
        if li % 2 == 0:
            e = li // 2
            yl, yc = even_mixer(hl, hc, ev_w_in[e], ev_w_out[e], lru_conv_w[e], lru_conv_b[e],
                                lru_wa[e], lru_ba[e], lru_wx[e], lru_bx[e], lru_lambda[e],
                                ret_decay_logit[e], ret_gn[e], cos1, sin1)
        else:
            o = li // 2
            yl, yc = odd_mixer(hl, hc, od_w_in[o], od_w_out[o], pool_w[o], pool_scale[o],
                               q_norm[o], k_norm[o], cos2, sin2, not last)
        xl = residual(xl, yl, norm_post[li, 1], mod_l[:, 1, 2], 1.0)
        if not last:
            xc = residual(xc, yc, norm_post[li, 1], mod_c[1, 2], 1.0)

        hl = modulate(xl, norm_pre[li, 2], mod_l[:, 2, 0], mod_l[:, 2, 1])
        xl = residual(xl, swiglu(hl, ffn_gate[li, 1], ffn_up[li, 1], ffn_down[li, 1]), norm_post[li, 2], mod_l[:, 2, 2], FFN_STEP)
        if not last:
            hc = modulate(xc, norm_pre[li, 2], mod_c[2, 0], mod_c[2, 1])
            xc = residual(xc, swiglu(hc, ffn_gate[li, 1], ffn_up[li, 1], ffn_down[li, 1]), norm_post[li, 2], mod_c[2, 2], FFN_STEP)
    return xl
```

```python
import contextlib
import numpy as np
import concourse.bass as bass
import concourse.mybir as mybir
from concourse.bass_utils import run_bass_kernel_spmd

F32 = mybir.dt.float32
BF16 = mybir.dt.bfloat16
AF = mybir.ActivationFunctionType
ALU = mybir.AluOpType
AX = mybir.AxisListType

ENGS = ("tensor", "vector", "scalar", "gpsimd", "sync")
EPOCH = 30000
DYN_STRIDE = 4096


def _m(name, *args, **kwargs):
    def call(e):
        return getattr(e, name)(*args, **kwargs)
    return call


class Buf:
    __slots__ = ("name", "lw", "rd", "sem", "cum")

    def __init__(self, name=""):
        self.name = name
        self.lw = None
        self.rd = []
        self.sem = None
        self.cum = 0


class Prog:
    def __init__(self, nc, same_engine_sync=True):
        self.nc = nc
        self.q = {e: [] for e in ENGS}
        self.cnt = {e: 0 for e in ENGS}
        self.waited = {e: {} for e in ENGS}
        self.semkeys = {}
        self.same = same_engine_sync
        self.ndma = 0
        self.dma_cum = {}

    def _signal_last(self, eng):
        q = self.q[eng]
        for ent in reversed(q):
            if ent[0] == "op":
                if ent[2] is None:
                    self.cnt[eng] += 1
                    c = self.cnt[eng]
                    key = ("e", eng, (c - 1) // EPOCH)
                    self.semkeys.setdefault(key, None)
                    ent[2] = (key, (c - 1) % EPOCH + 1)
                return ent[2]
        return None

    def _resolve(self, tok):
        if tok[0] == "lazy":
            ent = tok[2]
            if ent[2] is not None:
                return ent[2]
            eng = tok[1]
            t = self._signal_last(eng)
            ent[3] = t
            return t
        return tok

    def _wait(self, eng, tok):
        if tok is None:
            return
        if tok[0] == "lazy":
            src_eng = tok[1]
            if src_eng == eng and (eng == "tensor" or not self.same):
                return
            ent = tok[2]
            if ent[2] is None and ent[3] is not None:
                ctok = ent[3]
            else:
                ctok = self._resolve(tok)
        else:
            ctok = tok
        key, val = ctok
        w = self.waited[eng]
        if w.get(key, 0) >= val:
            return
        w[key] = val
        self.q[eng].append(["wait", key, val])

    def op(self, eng, fn, reads=(), writes=(), sig=False):
        for b in reads:
            self._wait(eng, b.lw)
        for b in writes:
            self._wait(eng, b.lw)
            for t in b.rd:
                self._wait(eng, t)
        ent = ["op", fn, None, None]
        self.q[eng].append(ent)
        if eng != "tensor" or sig:
            self.cnt[eng] += 1
            c = self.cnt[eng]
            key = ("e", eng, (c - 1) // EPOCH)
            self.semkeys.setdefault(key, None)
            ent[2] = (key, (c - 1) % EPOCH + 1)
        tok = ("lazy", eng, ent)
        for b in writes:
            b.lw = tok
            b.rd = []
        for b in reads:
            b.rd.append(tok)
            if len(b.rd) > 6:
                last = {}
                keep = []
                for t in b.rd:
                    if t[0] == "lazy":
                        last[t[1]] = t
                    else:
                        keep.append(t)
                b.rd = keep[-4:] + list(last.values())
        return tok

    def dma(self, eng, out, in_, reads=(), writes=(), sem=None, **kw):
        for b in reads:
            self._wait(eng, b.lw)
        sb = sem
        if sb.sem is None:
            sb.sem = ("d", self.ndma)
            self.ndma += 1
            self.semkeys[sb.sem] = None
        for b in writes:
            if not (b.lw is not None and b.lw[0] == sb.sem):
                self._wait(eng, b.lw)
            for t in b.rd:
                self._wait(eng, t)
        sb.cum += 16
        self.dma_cum[sb.sem] = sb.cum
        tok = (sb.sem, sb.cum)
        self.q[eng].append(["dmad" if (callable(out) or callable(in_)) else "dma", out, in_, kw, sb.sem])
        for b in writes:
            b.lw = tok
            b.rd = []
        for b in reads:
            b.rd.append(tok)
            if len(b.rd) > 8:
                b.rd = b.rd[-8:]
        return tok

    def cc(self, eng, fn, reads=(), writes=(), sem=None):
        sb = sem
        if sb.sem is None:
            sb.sem = ("d", self.ndma)
            self.ndma += 1
            self.semkeys[sb.sem] = None
        for b in reads:
            self._wait(eng, b.lw)
        for b in writes:
            self._wait(eng, b.lw)
            for t in b.rd:
                self._wait(eng, t)
        sb.cum += 16
        self.dma_cum[sb.sem] = sb.cum
        tok = (sb.sem, sb.cum)
        self.q[eng].append(["cc", fn, sb.sem])
        for b in writes:
            b.lw = tok
            b.rd = []
        for b in reads:
            b.rd.append(tok)
        return tok

    def wait_all(self, eng, bufs):
        for b in bufs:
            self._wait(eng, b.lw)
            for t in b.rd:
                self._wait(eng, t)

    def barrier(self):
        toks = []
        for e in ENGS:
            t = self._signal_last(e)
            if t is not None:
                toks.append(t)
        for key in list(self.semkeys):
            if key[0] == "d":
                toks.append((key, self.dma_cum[key]))
        for e in ENGS:
            for t in toks:
                self._wait(e, t)

    _uid = 0

    def build(self):
        nc = self.nc
        Prog._uid += 1
        for key in self.semkeys:
            self.semkeys[key] = nc.alloc_semaphore(name="s%d_" % Prog._uid + "_".join(str(k) for k in key))
        sems = self.semkeys

        def replay(eng_name, e):
            dyn = {}
            for ent in self.q[eng_name]:
                k = ent[0]
                if k == "op":
                    ins = ent[1](e)
                    if ent[2] is not None:
                        ins.then_inc(sems[ent[2][0]], 1)
                elif k == "wait":
                    e.wait_ge(sems[ent[1]], ent[2])
                elif k == "dmad":
                    if "hv" not in dyn:
                        dyn["hv"] = e.partition_id() & 1
                    o_ = ent[1](dyn["hv"]) if callable(ent[1]) else ent[1]
                    i_ = ent[2](dyn["hv"]) if callable(ent[2]) else ent[2]
                    e.dma_start(out=o_, in_=i_, **ent[3]).then_inc(sems[ent[4]], 16)
                elif k == "cc":
                    ent[1](e).then_inc(sems[ent[2]], 16)
                else:
                    e.dma_start(out=ent[1], in_=ent[2], **ent[3]).then_inc(sems[ent[4]], 16)

        with nc.Block() as block:
            @block.tensor
            def _(e):
                replay("tensor", e)

            @block.vector
            def _(e):
                replay("vector", e)

            @block.scalar
            def _(e):
                replay("scalar", e)

            @block.gpsimd
            def _(e):
                replay("gpsimd", e)

            @block.sync
            def _(e):
                replay("sync", e)

    def stats(self):
        return {e: (sum(1 for x in self.q[e] if x[0] != "wait"), sum(1 for x in self.q[e] if x[0] == "wait")) for e in ENGS}
D = 2048
NDC = 16
DFF = 5632
NFC = 44
B_, S_, LC = 4, 8192, 256
NLAT = 4096
NCTX = 128
NT = NLAT + NCTX
TT = 512
EPS = 1e-6


class K:
    def __init__(self, nc=None, tag=""):
        self.nc = nc if nc is not None else bass.Bass("TRN2", target_bir_lowering=False)
        self.es = contextlib.ExitStack()
        self.P = Prog(self.nc)
        self.nps = 0
        self._n = 0
        self.tag = tag

    def dscr(self, name, shape, dt=F32):
        return self.nc.dram_tensor(name, list(shape), dt).ap()

    def din(self, name, shape, dt=F32):
        return self.nc.dram_tensor(name, list(shape), dt, kind="ExternalInput").ap()

    def dout(self, name, shape, dt=F32):
        return self.nc.dram_tensor(name, list(shape), dt, kind="ExternalOutput").ap()

    def sb(self, name, shape, dt=F32):
        return self.es.enter_context(self.nc.sbuf_tensor("sb_" + self.tag + name, list(shape), dt))

    def ps(self, name, shape=(128, 512), dt=F32):
        self.nps += 1
        return self.es.enter_context(self.nc.psum_tensor("ps_" + self.tag + name, list(shape), dt))

    def uid(self, p="b"):
        self._n += 1
        return "%s%d" % (p, self._n)


class Ring:
    def __init__(self, k, name, n, shape, dt, psum=False):
        self.slots = []
        for i in range(n):
            t = k.ps("%s%d" % (name, i), shape, dt) if psum else k.sb("%s%d" % (name, i), shape, dt)
            self.slots.append((t, Buf("%s%d" % (name, i))))
        self.i = 0

    def next(self):
        s = self.slots[self.i % len(self.slots)]
        self.i += 1
        return s


def load_consts(k, names_shapes, dram):
    out = {}
    for name, shape in names_shapes:
        t = k.sb("c_" + name, shape, F32)
        b = Buf(name)
        k.P.dma("sync", t[:], dram[name], writes=[b], sem=b)
        out[name] = (t, b)
    return out


class RowLocal:
    def __init__(self, k, T):
        self.k = k
        P = k.P
        self.T = T
        self.x = k.sb("x", [128, NDC, T], F32); self.Bx = Buf("x")
        self.h = k.sb("h", [128, NDC, T], BF16); self.Bh = Buf("h")
        self.y = k.sb("y", [128, NDC, T], F32); self.By = Buf("y")
        self.act = None
        self.w16 = Ring(k, "w16_", 5, [128, 16, 128], BF16)
        self.w44 = None
        self.pmm = Ring(k, "pmm", 5, [128, 512], F32, psum=True)
        self.pst = Ring(k, "pst", 2, [128, 512], F32, psum=True)
        self.sq = Ring(k, "sq", 2, [128, T], BF16)
        self.tmp = Ring(k, "tmp", 3, [128, T], F32)
        self.sg = Ring(k, "sg", 2, [128, T], F32)
        self.rstd = k.sb("rstd", [128, T], F32); self.Brstd = Buf("rstd")
        self.rstd2 = k.sb("rstd2", [128, T], F32); self.Brstd2 = Buf("rstd2")
        self.stage = Ring(k, "stg", 4, [128, T], F32)
        self.ones = k.sb("ones", [128, 128], BF16); self.Bones = Buf("ones")
        P.op("gpsimd", _m("memset", self.ones[:], 1.0), writes=[self.Bones])
        self.flip = 0

    def alloc_ffn(self):
        k = self.k
        self.act = k.sb("act", [128, NFC, self.T], BF16); self.Bact = Buf("act")
        self.w44 = Ring(k, "w44_", 3, [128, 44, 128], BF16)

    def prep_mod(self, modv, Bmodv, npre, Bnpre, npost, Bnpost, nkinds, subs, wsteps):
        k = self.k; P = k.P
        self.A = k.sb("modA", [128, 2, 3, 16], F32)
        self.Bv = k.sb("modB", [128, 2, 3, 16], F32)
        self.C = k.sb("modC", [128, 2, 3, 16], F32)
        prep_mod_into(self, modv, Bmodv, npre, Bnpre, npost, Bnpost, nkinds, subs, wsteps)

    def rstd_from_psum(self, ps, Bps, out, Bout, T):
        P = self.k.P
        P.op("scalar", _m("activation", out=out[:, :T], in_=ps[:, :T], func=AF.Sqrt, bias=self.epsb[:, 0:1], scale=1.0 / D),
             reads=[Bps, self.Bepsb], writes=[Bout])
        P.op("vector", _m("reciprocal", out=out[:, :T], in_=out[:, :T]), reads=[Bout], writes=[Bout])

    def init_eps(self):
        k = self.k
        self.epsb = k.sb("epsb", [128, 1], F32); self.Bepsb = Buf("epsb")
        k.P.op("gpsimd", _m("memset", self.epsb[:], EPS), writes=[self.Bepsb])

    def norm_mod(self, kd, sub, T):
        k = self.k; P = k.P
        ps, Bps = self.pst.next()
        for dc in range(NDC):
            sq, Bsq = self.sq.next()
            P.op("scalar", _m("activation", out=sq[:, :T], in_=self.x[:, dc, :T], func=AF.Square),
                 reads=[self.Bx], writes=[Bsq])
            P.op("tensor", _m("matmul", ps[:, :T], lhsT=self.ones[:], rhs=sq[:, :T],
                                                                  start=(dc == 0), stop=(dc == NDC - 1)),
                 reads=[Bsq, self.Bones], writes=[Bps])
        self.rstd_from_psum(ps, Bps, self.rstd, self.Brstd, T)
        for dc in range(NDC):
            tmp, Bt = self.tmp.next()
            P.op("vector", _m("tensor_tensor", out=tmp[:, :T], in0=self.x[:, dc, :T], in1=self.rstd[:, :T], op=ALU.mult),
                 reads=[self.Bx, self.Brstd], writes=[Bt])
            P.op("scalar", _m("activation", out=self.h[:, dc, :T], in_=tmp[:, :T], func=AF.Identity,
                                                                 scale=self.A[:, kd, sub, dc:dc + 1], bias=self.Bv[:, kd, sub, dc:dc + 1]),
                 reads=[Bt, self.Bmod], writes=[self.Bh])

    def linear(self, rhs_of, Brhs, wdram, nk, nn, T, evac, ring=None):
        P = self.k.P
        ring = ring or (self.w16 if nk <= 16 else self.w44)
        pend = []
        LOOK = len(ring.slots) - 1

        def issue(n):
            wt, Bw = ring.next()
            P.dma("gpsimd", wt[:, :nk, :], wdram[n], writes=[Bw], sem=Bw, max_dma_last_dim=8192)
            return wt, Bw

        for n in range(min(LOOK, nn)):
            pend.append(issue(n))
        for n in range(nn):
            wt, Bw = pend.pop(0)
            ps, Bps = self.pmm.next()
            for kc in range(nk):
                P.op("tensor", _m("matmul", ps[:, :T], lhsT=wt[:, kc, :], rhs=rhs_of(kc),
                                                                      start=(kc == 0), stop=(kc == nk - 1)),
                     reads=[Bw, Brhs], writes=[Bps])
            if n + LOOK < nn:
                pend.append(issue(n + LOOK))
            evac(n, ps, Bps)

    def make_post_evac(self, T):
        P = self.k.P
        pst, Bpst = self.pst.next()
        state = {"prev": None}

        def flush_stat(last):
            if state["prev"] is None:
                return
            n, sq, Bsq = state["prev"]
            P.op("tensor", _m("matmul", pst[:, :T], lhsT=self.ones[:], rhs=sq[:, :T], start=(n == 0), stop=last),
                 reads=[Bsq, self.Bones], writes=[Bpst])
            state["prev"] = None

        def evac(n, ps, Bps):
            flush_stat(False)
            P.op("scalar", _m("activation", out=self.y[:, n, :T], in_=ps[:, :T], func=AF.Copy), reads=[Bps], writes=[self.By])
            sq, Bsq = self.sq.next()
            P.op("vector", _m("tensor_tensor", out=sq[:, :T], in0=self.y[:, n, :T], in1=self.y[:, n, :T], op=ALU.mult), reads=[self.By], writes=[Bsq])
            state["prev"] = (n, sq, Bsq)

        def finish(kd, sub):
            flush_stat(True)
            self.rstd_from_psum(pst, Bpst, self.rstd2, self.Brstd2, T)
            for dc in range(NDC):
                tmp, Bt = self.tmp.next()
                eng = "gpsimd" if dc % 4 == 3 else "vector"
                P.op(eng, _m("tensor_tensor", out=tmp[:, :T], in0=self.y[:, dc, :T], in1=self.rstd2[:, :T], op=ALU.mult),
                     reads=[self.By, self.Brstd2], writes=[Bt])
                P.op("vector", _m("scalar_tensor_tensor",
                    out=self.x[:, dc, :T], in0=tmp[:, :T], scalar=self.C[:, kd, sub, dc:dc + 1], in1=self.x[:, dc, :T],
                    op0=ALU.mult, op1=ALU.add), reads=[Bt, self.Bmod, self.Bx], writes=[self.Bx])
        return evac, finish

    def ffn_sublayer(self, kd, sub, T, wgu, wdn):
        P = self.k.P
        self.norm_mod(kd, sub, T)
        sgs = {}

        def evac_gu(n, ps, Bps):
            f = n // 2
            if n % 2 == 0:
                sg, Bsg = self.sg.next()
                P.op("scalar", _m("activation", out=sg[:, :T], in_=ps[:, :T], func=AF.Silu), reads=[Bps], writes=[Bsg])
                sgs[f] = (sg, Bsg)
            else:
                sg, Bsg = sgs.pop(f)
                P.op("vector", _m("tensor_tensor", out=self.act[:, f, :T], in0=sg[:, :T], in1=ps[:, :T], op=ALU.mult),
                     reads=[Bsg, Bps], writes=[self.Bact])

        self.linear(lambda kc: self.h[:, kc, :T], self.Bh, wgu, NDC, 2 * NFC, T, evac_gu)
        evac, finish = self.make_post_evac(T)
        self.linear(lambda kc: self.act[:, kc, :T], self.Bact, wdn, NFC, NDC, T, evac)
        finish(kd, sub)

    def mixer_out_sublayer(self, kd, sub, T, wout):
        evac, finish = self.make_post_evac(T)
        self.linear(lambda kc: self.h[:, kc, :T], self.Bh, wout, NDC, NDC, T, evac)
        finish(kd, sub)

    def proj_out(self, kd, sub, T, win, nn, proj_dram, t0, Bdst=None, route=None):
        P = self.k.P
        self.norm_mod(kd, sub, T)
        pj = proj_dram.rearrange("(c p) t -> c p t", p=128) if proj_dram is not None else None
        wr = [Bdst] if Bdst is not None else []

        def evac(n, ps, Bps):
            st, Bst = self.stage.next()
            if n % 2 == 0:
                P.op("scalar", _m("activation", out=st[:, :T], in_=ps[:, :T], func=AF.Copy), reads=[Bps], writes=[Bst])
            else:
                P.op("vector", _m("tensor_copy", out=st[:, :T], in_=ps[:, :T]), reads=[Bps], writes=[Bst])
            if route is None:
                P.dma("sync", pj[n, :, t0:t0 + T], st[:, :T], reads=[Bst], writes=wr, sem=Bst)
            else:
                for (dst, cols) in route(n):
                    P.dma("sync", dst, st[:, cols], reads=[Bst], writes=wr, sem=Bst)

        self.linear(lambda kc: self.h[:, kc, :T], self.Bh, win, NDC, nn, T, evac)

    def load_x(self, xdram, t0, T, dyn=None, Bsrc=None):
        P = self.k.P
        rd = [Bsrc] if Bsrc is not None else []
        if dyn is not None:
            in_ = lambda h1, j=dyn: xdram[j][bass.ds(h1, 1), :, :].rearrange("o (c p) t -> p (o c) t", p=128)
            P.dma("sync", self.x[:, :, :T], in_, reads=rd, writes=[self.Bx], sem=self.Bx)
            return
        src = xdram.rearrange("(c p) t -> p c t", p=128)
        for c4 in range(4):
            cs_ = slice(4 * c4, 4 * c4 + 4)
            P.dma("sync", self.x[:, cs_, :T], src[:, cs_, t0:t0 + T], reads=rd, writes=[self.Bx], sem=self.Bx)

    def store_x(self, xdram, t0, T, Bdst=None):
        P = self.k.P
        dst = xdram.rearrange("(c p) t -> p c t", p=128)
        wr = [Bdst] if Bdst is not None else []
        for c4 in range(4):
            P.dma("sync", dst[:, 4 * c4:4 * c4 + 4, t0:t0 + T], self.x[:, 4 * c4:4 * c4 + 4, :T], reads=[self.Bx], writes=wr, sem=self.Bx)

    def load_h(self, mdram, t0, T, Bsrc=None):
        P = self.k.P
        src = mdram.rearrange("(c p) t -> p c t", p=128)
        rd = [Bsrc] if Bsrc is not None else []
        for c4 in range(4):
            P.dma("scalar", self.h[:, 4 * c4:4 * c4 + 4, :T], src[:, 4 * c4:4 * c4 + 4, t0:t0 + T], reads=rd, writes=[self.Bh], sem=self.Bh)


def tiles_of(nlat, nctx, T):
    out = []
    t = 0
    while t < nlat:
        out.append((t, min(T, nlat - t), 0)); t += T
    t = nlat
    while t < nlat + nctx:
        out.append((t, min(T, nlat + nctx - t), 1)); t += T
    return out


def tile_w(w, nk, nn):
    return np.ascontiguousarray(w.reshape(nk, 128, nn, 128).transpose(2, 1, 0, 3))


def tile_gu(wg, wu):
    g = tile_w(wg, NDC, NFC); u = tile_w(wu, NDC, NFC)
    return np.ascontiguousarray(np.stack([g, u], axis=1).reshape(2 * NFC, 128, NDC, 128))


def vec16(v):
    v = np.asarray(v, np.float32)
    lead = v.shape[:-1]
    return np.ascontiguousarray(np.moveaxis(v.reshape(lead + (16, 128)), -1, 0))


NTOT = LC + S_
LW = 2048


def lru_tiles():
    ctx = [(0, LC, True, True)]
    lat = [(LC + i * LW, LW, i == 0, i == S_ // LW - 1) for i in range(S_ // LW)]
    return ctx, lat


def emit_B1(k, t, ncc=8):
    P = k.P
    proj, Bproj = t["proj"], t["Bproj"]
    cw, wab, bab, lam = t["cw"], t["wab"], t["bab"], t["lam"]
    out, Bout = t["mixA"], t["BmixA"]
    h0s = t["h0s"]
    Bh0s = Buf("h0s")
    if True:
        cs = load_consts(k, [("cw", [128, ncc, 5]), ("bab", [128, 2, 2, ncc]), ("lam", [128, 2, ncc])],
                         {"cw": cw, "bab": bab, "lam": lam})
        cwt, Bcw = cs["cw"]; babt, Bbab = cs["bab"]; lamt, Blam = cs["lam"]
        wt = k.sb("wab", [128, 2, 2, ncc, 128], BF16); Bwt = Buf("wab")
        P.dma("gpsimd", wt[:], wab, writes=[Bwt], sem=Bwt)
        c8 = k.sb("c8", [128, 2, ncc]); c16 = k.sb("c16", [128, 2, ncc]); Bc8 = Buf("c8")
        P.op("scalar", _m("activation", out=c8[:], in_=lamt[:], func=AF.Exp, scale=-1.0), reads=[Blam], writes=[Bc8])
        P.op("scalar", _m("activation", out=c8[:], in_=c8[:], func=AF.Ln, bias=1.0, scale=1.0), reads=[Bc8], writes=[Bc8])
        P.op("vector", _m("tensor_scalar", out=c16[:], in0=c8[:], scalar1=-16.0, scalar2=None, op0=ALU.mult), reads=[Bc8], writes=[Bc8])
        P.op("vector", _m("tensor_scalar", out=c8[:], in0=c8[:], scalar1=-8.0, scalar2=None, op0=ALU.mult), reads=[Bc8], writes=[Bc8])

        rt_r = Ring(k, "rt", 2, [128, LW + 3], F32)
        u_r = Ring(k, "u", 2, [128, LW], F32)
        ub_r = Ring(k, "ub", 3, [128, LW], BF16)
        rgt_r = Ring(k, "rgt", 2, [128, LW], F32)
        igt_r = Ring(k, "igt", 2, [128, LW], F32)
        a_r = Ring(k, "a", 2, [128, LW], F32)
        s_r = Ring(k, "s", 2, [128, LW], F32)
        h_r = Ring(k, "hh", 2, [128, LW], F32)
        g_r = Ring(k, "gg", 2, [128, LW], F32)
        h0_r = Ring(k, "h0", 2, [128, LW], F32)
        o_r = Ring(k, "ob", 2, [128, LW], BF16)
        st = k.sb("state", [128, 1], F32); Bst = Buf("state")
        pa = Ring(k, "pa", 3, [128, 512], F32, psum=True)
        px = Ring(k, "px", 3, [128, 512], F32, psum=True)
        ctx_t, lat_t = lru_tiles()
        def stage_a(cc, d, c0, W, s0, s1):
            rows = slice(cc * 128, (cc + 1) * 128)
            rrows = slice(1024 + cc * 128, 1024 + (cc + 1) * 128)
            rt, Brt = rt_r.next()
            lo = 0 if not s0 else 2
            hi = W + 3 if not s1 else W + 2
            if s0:
                P.op("gpsimd", _m("memset", rt[:, 0:2], 0.0), writes=[Brt])
            if s1:
                P.op("gpsimd", _m("memset", rt[:, W + 2:W + 3], 0.0), writes=[Brt])
            P.dma("sync", rt[:, lo:hi], proj[rrows, c0 - 2 + lo:c0 - 2 + hi], reads=[Bproj], writes=[Brt], sem=Brt)
            u, Bu = u_r.next()
            P.op("vector", _m("tensor_scalar",
                out=u[:, :W], in0=rt[:, 0:W], scalar1=cwt[:, cc, 0:1], scalar2=cwt[:, cc, 4:5], op0=ALU.mult, op1=ALU.add),
                reads=[Brt, Bcw], writes=[Bu])
            for kk in range(1, 4):
                P.op("vector", _m("scalar_tensor_tensor",
                    out=u[:, :W], in0=rt[:, kk:kk + W], scalar=cwt[:, cc, kk:kk + 1], in1=u[:, :W], op0=ALU.mult, op1=ALU.add),
                    reads=[Brt, Bcw, Bu], writes=[Bu])
            ub, Bub = ub_r.next()
            P.op("scalar", _m("activation", out=ub[:, :W], in_=u[:, :W], func=AF.Copy), reads=[Bu], writes=[Bub])
            rgt, Brg = rgt_r.next(); igt, Big = igt_r.next()
            for s in range(0, W, 512):
                n = min(512, W - s)
                psa, Bpa = pa.next(); psx, Bpx = px.next()
                P.op("tensor", _m("matmul",
                    psa[:, :n], lhsT=wt[:, 0, d, cc, :], rhs=ub[:, s:s + n], start=True, stop=True), reads=[Bwt, Bub], writes=[Bpa])
                P.op("tensor", _m("matmul",
                    psx[:, :n], lhsT=wt[:, 1, d, cc, :], rhs=ub[:, s:s + n], start=True, stop=True), reads=[Bwt, Bub], writes=[Bpx])
                P.op("scalar", _m("activation",
                    out=rgt[:, s:s + n], in_=psa[:, :n], func=AF.Sigmoid, bias=babt[:, 0, d, cc:cc + 1], scale=1.0),
                    reads=[Bpa, Bbab], writes=[Brg])
                P.op("scalar", _m("activation",
                    out=igt[:, s:s + n], in_=psx[:, :n], func=AF.Sigmoid, bias=babt[:, 1, d, cc:cc + 1], scale=1.0),
                    reads=[Bpx, Bbab], writes=[Big])
            a, Ba = a_r.next(); sq, Bs = s_r.next()
            P.op("scalar", _m("activation",
                out=a[:, :W], in_=rgt[:, :W], func=AF.Exp, scale=c8[:, d, cc:cc + 1]), reads=[Brg, Bc8], writes=[Ba])
            P.op("scalar", _m("activation",
                out=sq[:, :W], in_=rgt[:, :W], func=AF.Exp, scale=c16[:, d, cc:cc + 1]), reads=[Brg, Bc8], writes=[Bs])
            P.op("scalar", _m("activation", out=sq[:, :W], in_=sq[:, :W], func=AF.Sqrt, bias=1.0, scale=-1.0),
                 reads=[Bs], writes=[Bs])
            return dict(rows=rows, u=u, Bu=Bu, igt=igt, Big=Big, a=a, Ba=Ba, sq=sq, Bs=Bs)

        def stage_b(cc, d, c0, W, first, v):
            rows, u, Bu, igt, Big, a, Ba, sq, Bs = v['rows'], v['u'], v['Bu'], v['igt'], v['Big'], v['a'], v['Ba'], v['sq'], v['Bs']
            if first:
                P.op("gpsimd", _m("memset", st[:], 0.0), writes=[Bst])
            P.op("vector", _m("tensor_tensor", out=igt[:, :W], in0=igt[:, :W], in1=u[:, :W], op=ALU.mult),
                 reads=[Big, Bu], writes=[Big])
            P.op("gpsimd", _m("tensor_tensor", out=igt[:, :W], in0=igt[:, :W], in1=sq[:, :W], op=ALU.mult),
                 reads=[Big, Bs], writes=[Big])
            h, Bhh = h_r.next()
            if d == 0:
                P.op("vector", _m("tensor_tensor_scan",
                    out=h[:, :W], data0=a[:, :W], data1=igt[:, :W], initial=st[:, 0:1], op0=ALU.mult, op1=ALU.add),
                    reads=[Ba, Big, Bst], writes=[Bhh])
                P.op("vector", _m("tensor_copy", out=st[:, 0:1], in_=h[:, W - 1:W]), reads=[Bhh], writes=[Bst])
                P.dma("sync", h0s[rows, c0:c0 + W], h[:, :W], reads=[Bhh], writes=[Bh0s], sem=Bhh)
            else:
                P.op("vector", _m("tensor_tensor_scan",
                    out=h[:, W - 1::-1] if False else h[:, 0:W][:, ::-1], data0=a[:, 0:W][:, ::-1], data1=igt[:, 0:W][:, ::-1],
                    initial=st[:, 0:1], op0=ALU.mult, op1=ALU.add),
                    reads=[Ba, Big, Bst], writes=[Bhh])
                P.op("vector", _m("tensor_copy", out=st[:, 0:1], in_=h[:, 0:1]), reads=[Bhh], writes=[Bst])
                h0, Bh0 = h0_r.next(); g, Bg = g_r.next()
                P.dma("scalar", h0[:, :W], h0s[rows, c0:c0 + W], reads=[Bh0s], writes=[Bh0], sem=Bh0)
                P.dma("scalar", g[:, :W], proj[rows, c0:c0 + W], reads=[Bproj], writes=[Bg], sem=Bg)
                P.op("gpsimd", _m("tensor_tensor", out=h0[:, :W], in0=h0[:, :W], in1=h[:, :W], op=ALU.add),
                     reads=[Bhh, Bh0], writes=[Bh0])
                P.op("scalar", _m("activation", out=g[:, :W], in_=g[:, :W], func=AF.Gelu_apprx_tanh), reads=[Bg], writes=[Bg])
                ob, Bob = o_r.next()
                P.op("vector", _m("tensor_tensor", out=ob[:, :W], in0=g[:, :W], in1=h0[:, :W], op=ALU.mult),
                     reads=[Bg, Bh0], writes=[Bob])
                P.dma("sync", out[rows, c0:c0 + W], ob[:, :W], reads=[Bob], writes=[Bout], sem=Bob)

        jobs = []
        for cc in range(ncc):
            for d in range(2):
                order = ctx_t + lat_t if d == 0 else ctx_t + lat_t[::-1]
                for ti, (c0, W, s0, s1) in enumerate(order):
                    jobs.append((cc, d, c0, W, s0, s1, ti == 0))
        prev = None
        for (cc, d, c0, W, s0, s1, first) in jobs:
            cur = (cc, d, c0, W, first, stage_a(cc, d, c0, W, s0, s1))
            if prev is not None:
                stage_b(*prev)
            prev = cur
        stage_b(*prev)
RC = 128


def ret_tiles():
    ctx = [(0, LC, False, 0)]
    lat = [(LC + i * 512, 512, True, i * 512) for i in range(S_ // 512)]
    return ctx, lat


def host_ret_consts():
    j = np.arange(128, dtype=np.float32)[:, None]
    i = np.arange(128, dtype=np.float32)[None, :]
    dmat = np.stack([np.maximum(i - j, 0) + 0 * j, (i >= j).astype(np.float32), np.maximum(j - i, 0) + 0 * i, (j >= i).astype(np.float32)], 1)
    iq = np.stack([np.broadcast_to(i + 1.0, (128, 128)), np.broadcast_to(RC - i, (128, 128))], 1)
    jk = np.concatenate([RC - 1.0 - j, j], 1)
    return {"dmat": np.ascontiguousarray(dmat, np.float32), "iq": np.ascontiguousarray(iq, np.float32),
            "jk": np.ascontiguousarray(jk, np.float32), "ident": np.eye(128, dtype=np.float32)}


def host_rot1_tables():
    n_r = 128
    f_r = (10000.0 ** (-np.arange(n_r, dtype=np.float32) / n_r)).astype(np.float32)
    ang = np.arange(S_, dtype=np.float32)[:, None] * f_r
    return np.ascontiguousarray(np.cos(ang).T.astype(np.float32)), np.ascontiguousarray(np.sin(ang).T.astype(np.float32))


def emit_B2(k, t, hp):
    P = k.P
    proj, Bproj = t["proj"], t["Bproj"]
    cosT, sinT = t["cos1T"], t["sin1T"]
    dmat, iq, jk, ident = t["dmat"], t["iq"], t["jk"], t["ident"]
    dlb, gng = t["dlb"][:, :, 2 * hp:2 * hp + 2], t["gng"][:, 2 * hp:2 * hp + 2, :]
    out, Bout = t["mixA"], t["BmixA"]
    o0s = t["o0s"]; Bo0s = Buf("o0s")
    src = [proj[2048 + a * 1024 + hp * 512:2048 + a * 1024 + (hp + 1) * 512, :].rearrange("(a p) t -> p a t", p=128) for a in range(4)]
    o0v = o0s.rearrange("(a p) t -> p a t", p=128)
    outv = out[1024 + hp * 512:1024 + (hp + 1) * 512, :].rearrange("(a p) t -> p a t", p=128)
    if True:
        cs = load_consts(k, [("dmat", [128, 4, 128]), ("iq", [128, 2, 128]), ("jk", [128, 2]), ("ident", [128, 128]),
                             ("dlb", [128, 2, 2]), ("gng", [128, 2, 2])],
                         {"dmat": dmat, "iq": iq, "jk": jk, "ident": ident, "dlb": dlb, "gng": gng})
        dm, Bdm = cs["dmat"]; iqt, Biq = cs["iq"]; jkt, Bjk = cs["jk"]; idt, Bid = cs["ident"]; dl, Bdl = cs["dlb"]; gn, Bgn = cs["gng"]
        lg = k.sb("lg", [128, 2, 2]); Blg = Buf("lg")
        P.op("scalar", _m("activation", out=lg[:], in_=dl[:], func=AF.Exp, scale=-1.0), reads=[Bdl], writes=[Blg])
        P.op("scalar", _m("activation", out=lg[:], in_=lg[:], func=AF.Ln, bias=1.0, scale=1.0), reads=[Blg], writes=[Blg])
        P.op("vector", _m("tensor_scalar", out=lg[:], in0=lg[:], scalar1=-1.0, scalar2=None, op0=ALU.mult), reads=[Blg], writes=[Blg])
        mask = k.sb("mask", [128, 2, 2, 128], BF16); mtmp = k.sb("mtmp", [128, 128]); Bmask = Buf("mask"); Bmt = Buf("mtmp")
        qdec = k.sb("qdec", [128, 2, 2, 512], BF16); Bqdec = Buf("qdec")
        kdec = k.sb("kdec", [128, 2, 2]); sdec = k.sb("sdec", [128, 2, 2]); Bkd = Buf("kdec")
        c128 = k.sb("c128", [128, 1]); Bc128 = Buf("c128")
        P.op("gpsimd", _m("memset", c128[:], float(RC)), writes=[Bc128])
        for d in range(2):
            for hl in range(2):
                sc = lg[:, d, hl:hl + 1]
                P.op("scalar", _m("activation", out=mtmp[:], in_=dm[:, 2 * d, :], func=AF.Exp, scale=sc),
                     reads=[Bdm, Blg, Bmt], writes=[Bmt])
                P.op("vector", _m("tensor_tensor", out=mask[:, d, hl, :], in0=mtmp[:], in1=dm[:, 2 * d + 1, :], op=ALU.mult),
                     reads=[Bmt, Bdm], writes=[Bmask])
                P.op("scalar", _m("activation", out=mtmp[:], in_=iqt[:, d, :], func=AF.Exp, scale=sc),
                     reads=[Biq, Blg, Bmt], writes=[Bmt])
                for rep in range(4):
                    P.op("vector", _m("tensor_copy", out=qdec[:, d, hl, rep * 128:(rep + 1) * 128], in_=mtmp[:]),
                         reads=[Bmt], writes=[Bqdec])
                P.op("scalar", _m("activation", out=kdec[:, d, hl:hl + 1], in_=jkt[:, d:d + 1], func=AF.Exp, scale=sc),
                     reads=[Bjk, Blg], writes=[Bkd])
                P.op("scalar", _m("activation", out=sdec[:, d, hl:hl + 1], in_=c128[:], func=AF.Exp, scale=sc),
                     reads=[Bc128, Blg], writes=[Bkd])
        P.op("vector", _m("tensor_scalar", out=kdec[:], in0=kdec[:], scalar1=float(RET_KSCALE), scalar2=None, op0=ALU.mult),
             reads=[Bkd], writes=[Bkd])
        ones32 = k.sb("ones32", [128, 128]); Bo32 = Buf("ones32")
        P.op("gpsimd", _m("memset", ones32[:], 1.0), writes=[Bo32])
        epsb = k.sb("epsb", [128, 1]); Beps = Buf("eps")
        P.op("gpsimd", _m("memset", epsb[:], EPS), writes=[Beps])

        q_r = Ring(k, "q", 2, [128, 4, 512], F32); k_r = Ring(k, "k", 2, [128, 4, 512], F32); v_r = Ring(k, "v", 2, [128, 4, 512], F32)
        cs_r = Ring(k, "cs", 2, [128, 2, 512], F32)
        qb_r = Ring(k, "qb", 2, [128, 4, 512], BF16); kb_r = Ring(k, "kb", 2, [128, 4, 512], BF16)
        kr_r = Ring(k, "kr", 2, [128, 4, 512], F32)
        qd_r = Ring(k, "qd", 2, [128, 4, 512], BF16)
        vtm_r = Ring(k, "vtm", 2, [128, 4, 512], BF16); ktm_r = Ring(k, "ktm", 2, [128, 4, 512], BF16)
        t_r = Ring(k, "rt", 4, [128, 512], F32)
        msk_r = Ring(k, "msk", 6, [128, 128], BF16)
        S32r = [Ring(k, "S32_%d_" % h, 2, [128, 2, 256], F32) for h in range(2)]
        S32 = [None, None]; BS32 = [None, None]
        Sb_r = [Ring(k, "Sb%d_" % h, 4, [128, 2, 256], BF16) for h in range(2)]
        ost_r = Ring(k, "ost", 2, [128, 2, 512], F32)
        o0_r = Ring(k, "o0", 2, [128, 2, 512], F32)
        ol_r = Ring(k, "ol", 2, [128, 4, 512], F32)
        sq_r = Ring(k, "osq", 2, [128, 512], F32)
        mu = k.sb("mu", [128, 512]); Bmu = Buf("mu"); var = k.sb("var", [128, 512]); Bvar = Buf("var")
        y_r = Ring(k, "yb", 2, [128, 2, 512], BF16)
        psc = k.ps("sc"); Bsc = [Buf("sc%d" % i) for i in range(4)]
        po = [[k.ps("po%d%d" % (h, e_)) for e_ in range(2)] for h in range(2)]
        Bpo = [[Buf("po%d%d" % (h, e_)) for e_ in range(2)] for h in range(2)]
        pu = k.ps("pu"); Bpu = Buf("pu")
        ptv = k.ps("ptv"); Bptv = Buf("ptv"); ptk = k.ps("ptk"); Bptk = Buf("ptk")
        pst1, Bp1, pst2, Bp2 = ptv, Bptv, ptk, Bptk

        def eng_alt(i):
            return "vector" if i % 2 == 0 else "gpsimd"

        ctx_t, lat_t = ret_tiles()
        Sb_cur = [None, None]

        def stage_a(d, c0, W, rot, pos0):
            nch = W // RC
            q, Bq = q_r.next(); kk_, Bk = k_r.next(); v, Bv = v_r.next()
            P.dma("sync", q[:, :, :W], src[0][:, :, c0:c0 + W], reads=[Bproj], writes=[Bq], sem=Bq)
            P.dma("scalar", kk_[:, :, :W], src[1][:, :, c0:c0 + W], reads=[Bproj], writes=[Bk], sem=Bk)
            P.dma("sync", v[:, :, :W], src[2][:, :, c0:c0 + W], reads=[Bproj], writes=[Bv], sem=Bv)
            qb, Bqb = qb_r.next(); kb, Bkb = kb_r.next(); kr, Bkr = kr_r.next()
            if rot:
                cst, Bcs = cs_r.next()
                P.dma("scalar", cst[:, 0, :W], cosT[:, pos0:pos0 + W], writes=[Bcs], sem=Bcs)
                P.dma("scalar", cst[:, 1, :W], sinT[:, pos0:pos0 + W], writes=[Bcs], sem=Bcs)
                n = 0
                for (srct, Bsrc, dst, Bdst) in ((q, Bq, qb, Bqb), (kk_, Bk, kr, Bkr)):
                    for hl in range(2):
                        x1 = srct[:, 2 * hl, :W]; x2 = srct[:, 2 * hl + 1, :W]
                        t1, Bt1 = t_r.next(); t2, Bt2 = t_r.next()
                        P.op(eng_alt(n), _m("tensor_tensor", out=t1[:, :W], in0=x1, in1=cst[:, 0, :W], op=ALU.mult), reads=[Bsrc, Bcs], writes=[Bt1]); n += 1
                        P.op(eng_alt(n), _m("tensor_tensor", out=t2[:, :W], in0=x2, in1=cst[:, 1, :W], op=ALU.mult), reads=[Bsrc, Bcs], writes=[Bt2]); n += 1
                        P.op(eng_alt(n), _m("tensor_tensor", out=dst[:, 2 * hl, :W], in0=t1[:, :W], in1=t2[:, :W], op=ALU.subtract), reads=[Bt1, Bt2], writes=[Bdst]); n += 1
                        t3, Bt3 = t_r.next(); t4, Bt4 = t_r.next()
                        P.op(eng_alt(n), _m("tensor_tensor", out=t3[:, :W], in0=x1, in1=cst[:, 1, :W], op=ALU.mult), reads=[Bsrc, Bcs], writes=[Bt3]); n += 1
                        P.op(eng_alt(n), _m("tensor_tensor", out=t4[:, :W], in0=x2, in1=cst[:, 0, :W], op=ALU.mult), reads=[Bsrc, Bcs], writes=[Bt4]); n += 1
                        P.op(eng_alt(n), _m("tensor_tensor", out=dst[:, 2 * hl + 1, :W], in0=t3[:, :W], in1=t4[:, :W], op=ALU.add), reads=[Bt3, Bt4], writes=[Bdst]); n += 1
                krs, Bkrs = kr, Bkr
            else:
                P.op("scalar", _m("activation", out=qb[:, :, :W], in_=q[:, :, :W], func=AF.Copy), reads=[Bq], writes=[Bqb])
                krs, Bkrs = kk_, Bk
            P.op("scalar", _m("activation", out=kb[:, :, :W], in_=krs[:, :, :W], func=AF.Copy, scale=float(RET_KSCALE)),
                 reads=[Bkrs], writes=[Bkb])
            qd, Bqd = qd_r.next()
            for hl in range(2):
                for dch in range(2):
                    P.op(eng_alt(hl + dch), _m("tensor_tensor",
                        out=qd[:, 2 * hl + dch, :W], in0=qb[:, 2 * hl + dch, :W], in1=qdec[:, d, hl, :W], op=ALU.mult),
                        reads=[Bqb, Bqdec], writes=[Bqd])
            vtm, Bvtm = vtm_r.next(); ktm, Bktm = ktm_r.next()
            for c in range(nch):
                for a in range(4):
                    P.op("tensor", _m("transpose", out=ptv[:, a * 128:(a + 1) * 128], in_=v[:, a, c * 128:(c + 1) * 128], identity=idt[:]),
                         reads=[Bv, Bid], writes=[Bptv])
                P.op("scalar", _m("activation", out=vtm[:, c, :], in_=ptv[:], func=AF.Copy), reads=[Bptv], writes=[Bvtm])
                for a in range(4):
                    P.op("tensor", _m("transpose", out=ptk[:, a * 128:(a + 1) * 128], in_=krs[:, a, c * 128:(c + 1) * 128], identity=idt[:]),
                         reads=[Bkrs, Bid], writes=[Bptk])
                for hl in range(2):
                    P.op("vector", _m("tensor_scalar",
                        out=ktm[:, c, hl * 256:(hl + 1) * 256], in0=ptk[:, hl * 256:(hl + 1) * 256], scalar1=kdec[:, d, hl:hl + 1], scalar2=None, op0=ALU.mult),
                        reads=[Bptk, Bkd], writes=[Bktm])
            return dict(nch=nch, qb=qb, Bqb=Bqb, kb=kb, Bkb=Bkb, qd=qd, Bqd=Bqd, vtm=vtm, Bvtm=Bvtm, ktm=ktm, Bktm=Bktm)

        def stage_b(d, c0, W, first, v):
            nch, qb, Bqb, kb, Bkb, qd, Bqd = v['nch'], v['qb'], v['Bqb'], v['kb'], v['Bkb'], v['qd'], v['Bqd']
            vtm, Bvtm, ktm, Bktm = v['vtm'], v['Bvtm'], v['ktm'], v['Bktm']
            if first:
                for hl in range(2):
                    S32[hl], BS32[hl] = S32r[hl].next()
                    P.op("gpsimd", _m("memset", S32[hl][:], 0.0), writes=[BS32[hl]])
                    sb_, Bsb_ = Sb_r[hl].next()
                    P.op("gpsimd", _m("memset", sb_[:], 0.0), writes=[Bsb_])
                    Sb_cur[hl] = (sb_, Bsb_)
            chunks = list(range(nch)) if d == 0 else list(range(nch))[::-1]

            def front(c):
                res = []
                for hl in range(2):
                    cs_ = slice(c * 128, (c + 1) * 128)
                    sl_ = (2 * c + hl) % 4
                    ss_ = slice(sl_ * 128, (sl_ + 1) * 128)
                    for dch in range(2):
                        P.op("tensor", _m("matmul",
                            psc[:, ss_], lhsT=kb[:, 2 * hl + dch, cs_], rhs=qb[:, 2 * hl + dch, cs_], start=(dch == 0), stop=(dch == 1)),
                            reads=[Bkb, Bqb], writes=[Bsc[sl_]], sig=(dch == 1))
                    for dch in range(2):
                        P.op("tensor", _m("matmul",
                            pu[:, dch * 256:(dch + 1) * 256], lhsT=ktm[:, c, hl * 256 + dch * 128:hl * 256 + (dch + 1) * 128],
                            rhs=vtm[:, c, hl * 256:(hl + 1) * 256], start=True, stop=True),
                            reads=[Bktm, Bvtm], writes=[Bpu], sig=(dch == 1))
                    mk, Bmk = msk_r.next()
                    P.op("vector", _m("tensor_tensor", out=mk[:], in0=psc[:, ss_], in1=mask[:, d, hl, :], op=ALU.mult),
                         reads=[Bsc[sl_], Bmask], writes=[Bmk])
                    sbt, Bsbt = Sb_cur[hl]
                    So, BSo = S32[hl], BS32[hl]
                    Sn, BSn = S32r[hl].next()
                    P.op("vector", _m("scalar_tensor_tensor",
                        out=Sn[:].rearrange("p a b -> p (a b)"), in0=So[:].rearrange("p a b -> p (a b)"), scalar=sdec[:, d, hl:hl + 1],
                        in1=pu[:], op0=ALU.mult, op1=ALU.add), reads=[Bpu, Bkd, BSo], writes=[BSn])
                    S32[hl], BS32[hl] = Sn, BSn
                    nsb, Bnsb = Sb_r[hl].next()
                    P.op("scalar", _m("activation", out=nsb[:], in_=S32[hl][:], func=AF.Copy), reads=[BS32[hl]], writes=[Bnsb])
                    Sb_cur[hl] = (nsb, Bnsb)
                    res.append((mk, Bmk, sbt, Bsbt))
                return res

            def back(c, res):
                cs_ = slice(c * 128, (c + 1) * 128)
                for hl in range(2):
                    mk, Bmk, sbt, Bsbt = res[hl]
                    for ech in range(2):
                        P.op("tensor", _m("matmul",
                            po[hl][ech][:, cs_], lhsT=vtm[:, c, (2 * hl + ech) * 128:(2 * hl + ech + 1) * 128], rhs=mk[:], start=True, stop=False),
                            reads=[Bvtm, Bmk], writes=[Bpo[hl][ech]])
                        for dch in range(2):
                            P.op("tensor", _m("matmul",
                                po[hl][ech][:, cs_], lhsT=sbt[:, dch, ech * 128:(ech + 1) * 128], rhs=qd[:, 2 * hl + dch, cs_], start=False, stop=(dch == 1)),
                                reads=[Bsbt, Bqd], writes=[Bpo[hl][ech]])

            pending = front(chunks[0])
            for ci, c in enumerate(chunks):
                nxt_ = front(chunks[ci + 1]) if ci + 1 < len(chunks) else None
                back(c, pending)
                pending = nxt_
            for hl in range(2):
                if d == 0:
                    ost, Bost = ost_r.next()
                    for ech in range(2):
                        P.op("scalar" if ech == 0 else "vector",
                             (_m("activation", out=ost[:, ech, :W], in_=po[hl][ech][:, :W], func=AF.Copy)) if ech == 0 else
                             (_m("tensor_copy", out=ost[:, ech, :W], in_=po[hl][ech][:, :W])),
                             reads=[Bpo[hl][ech]], writes=[Bost])
                    P.dma("sync", o0v[:, 2 * hl:2 * hl + 2, c0:c0 + W], ost[:, :, :W], reads=[Bost], writes=[Bo0s], sem=Bost)
                else:
                    o0, Bo0 = o0_r.next()
                    P.dma("sync", o0[:, :, :W], o0v[:, 2 * hl:2 * hl + 2, c0:c0 + W], reads=[Bo0s], writes=[Bo0], sem=Bo0)
                    if hl == 0:
                        olt, Bol = ol_r.next()
                        P.dma("scalar", olt[:, :, :W], src[3][:, :, c0:c0 + W], reads=[Bproj], writes=[Bol], sem=Bol)
                    for ech in range(2):
                        P.op("vector", _m("tensor_tensor", out=o0[:, ech, :W], in0=o0[:, ech, :W], in1=po[hl][ech][:, :W], op=ALU.add),
                             reads=[Bo0, Bpo[hl][ech]], writes=[Bo0])
                    for ech in range(2):
                        P.op("tensor", _m("matmul", pst1[:, :W], lhsT=ones32[:], rhs=o0[:, ech, :W], start=(ech == 0), stop=(ech == 1)),
                             reads=[Bo0, Bo32], writes=[Bp1])
                    for ech in range(2):
                        sq, Bsq = sq_r.next()
                        P.op("scalar", _m("activation", out=sq[:, :W], in_=o0[:, ech, :W], func=AF.Square), reads=[Bo0], writes=[Bsq])
                        P.op("tensor", _m("matmul", pst2[:, :W], lhsT=ones32[:], rhs=sq[:, :W], start=(ech == 0), stop=(ech == 1)),
                             reads=[Bsq, Bo32], writes=[Bp2])
                    P.op("vector", _m("tensor_scalar", out=mu[:, :W], in0=pst1[:, :W], scalar1=1.0 / 256, scalar2=None, op0=ALU.mult), reads=[Bp1], writes=[Bmu])
                    P.op("vector", _m("tensor_tensor", out=var[:, :W], in0=mu[:, :W], in1=mu[:, :W], op=ALU.mult), reads=[Bmu], writes=[Bvar])
                    P.op("vector", _m("scalar_tensor_tensor", out=var[:, :W], in0=pst2[:, :W], scalar=1.0 / 256, in1=var[:, :W], op0=ALU.mult, op1=ALU.subtract),
                         reads=[Bp2, Bvar], writes=[Bvar])
                    P.op("scalar", _m("activation", out=var[:, :W], in_=var[:, :W], func=AF.Sqrt, bias=epsb[:, 0:1], scale=1.0), reads=[Bvar, Beps], writes=[Bvar])
                    P.op("vector", _m("reciprocal", out=var[:, :W], in_=var[:, :W]), reads=[Bvar], writes=[Bvar])
                    yb, Byb = y_r.next()
                    for ech in range(2):
                        P.op("vector", _m("tensor_tensor", out=o0[:, ech, :W], in0=o0[:, ech, :W], in1=mu[:, :W], op=ALU.subtract), reads=[Bo0, Bmu], writes=[Bo0])
                        P.op("gpsimd", _m("tensor_tensor", out=o0[:, ech, :W], in0=o0[:, ech, :W], in1=var[:, :W], op=ALU.mult), reads=[Bo0, Bvar], writes=[Bo0])
                        P.op("scalar", _m("activation", out=olt[:, 2 * hl + ech, :W], in_=olt[:, 2 * hl + ech, :W], func=AF.Silu), reads=[Bol], writes=[Bol])
                        P.op("vector", _m("scalar_tensor_tensor",
                            out=yb[:, ech, :W], in0=o0[:, ech, :W], scalar=gn[:, hl, ech:ech + 1], in1=olt[:, 2 * hl + ech, :W], op0=ALU.mult, op1=ALU.mult),
                            reads=[Bo0, Bgn, Bol], writes=[Byb])
                    P.dma("sync", outv[:, 2 * hl:2 * hl + 2, c0:c0 + W], yb[:, :, :W], reads=[Byb], writes=[Bout], sem=Byb)

        jobs = []
        for d in range(2):
            order = ctx_t + (lat_t if d == 0 else lat_t[::-1])
            for ti, (c0, W, rot, pos0) in enumerate(order):
                jobs.append((d, c0, W, rot, pos0, ti == 0))
        prev = None
        for (d, c0, W, rot, pos0, first) in jobs:
            cur = (d, c0, W, first, stage_a(d, c0, W, rot, pos0))
            if prev is not None:
                stage_b(*prev)
            prev = cur
        stage_b(*prev)

RET_KSCALE = 256 ** -0.5

POOL_WINDOWS = (2, 4, 8, 16)
NKC = NTOT // 128
ATT_SCALE = 128 ** -0.5


def host_rot2_tables():
    rows = S_ // 64
    row = np.repeat(np.arange(rows, dtype=np.float32), 64)
    col = np.tile(np.arange(64, dtype=np.float32), rows)
    n_ax = 32
    f_ax = (10000.0 ** (-np.arange(n_ax, dtype=np.float32) / n_ax)).astype(np.float32)
    ang = np.concatenate([row[:, None] * f_ax, col[:, None] * f_ax], -1)
    c = np.cos(ang).astype(np.float32).T; s = np.sin(ang).astype(np.float32).T
    cosF = np.concatenate([c, c], 0); sinS = np.concatenate([-s, s], 0)
    pm = np.zeros((128, 128), np.float32)
    for dp in range(128):
        pm[(dp + 64) % 128, dp] = 1.0
    return np.ascontiguousarray(cosF), np.ascontiguousarray(sinS), pm


def host_pool_icnt(half):
    t = np.arange(S_)
    out = np.zeros((128, 4, NLAT), np.float32)
    for g, w in enumerate(POOL_WINDOWS):
        lo = np.clip(t - w // 2, 0, S_); hi = np.clip(t + w // 2, 0, S_)
        out[:, g, :] = (1.0 / (hi - lo).astype(np.float32))[None, half * NLAT:(half + 1) * NLAT]
    return out


def host_pool_slice(pool_rows, half):
    pad = np.zeros((512, S_ + 16), np.float32)
    pad[:, 8:8 + S_] = pool_rows
    return np.ascontiguousarray(pad[:, half * NLAT:half * NLAT + NLAT + 16].reshape(4, 128, NLAT + 16))


def emit_D(k, t):
    P = k.P
    NKV, NQH, NQT = 4, 12, NLAT // 512
    poolS, BpoolS = t["poolS"], t["BpoolS"]
    kvS, BkvS = t["kvS"], t["BkvS"]
    qS, BqS = t["qS"], t["BqS"]
    cosF, sinS, pm, ident = t["cosF"], t["sinS"], t["pm"], t["ident"]
    cosQ, sinQ = t["cosQ"], t["sinQ"]
    icnt, hmask = t["icnt"], t["hmask"]
    pw, psc_, qkg = t["pw"], t["pscale"], t["qkg"]
    mixB, BmixB = t["mixB"], t["BmixB"]
    if True:
        cs = load_consts(k, [("pm", [128, 128]), ("ident", [128, 128]), ("pscale", [128, 4]), ("qkg", [128, 2]), ("hmask", [128, 16])],
                         {"pm": pm, "ident": ident, "pscale": psc_, "qkg": qkg, "hmask": hmask})
        hmt, Bhm = cs["hmask"]
        pmt, Bpm = cs["pm"]; idt, Bid = cs["ident"]; pst_, Bpsc = cs["pscale"]; qkgt, Bqkg = cs["qkg"]
        pwt = k.sb("pwt", [128, 4, 128], BF16); Bpw = Buf("pw")
        P.dma("gpsimd", pwt[:], pw, writes=[Bpw], sem=Bpw)
        ones32 = k.sb("ones32", [128, 128]); Bo32 = Buf("o32")
        P.op("gpsimd", _m("memset", ones32[:], 1.0), writes=[Bo32])
        onesb = k.sb("onesb", [128, 128], BF16); Bob = Buf("ob")
        P.op("gpsimd", _m("memset", onesb[:], 1.0), writes=[Bob])
        epsb = k.sb("epsb", [128, 1]); Beps = Buf("eps")
        P.op("gpsimd", _m("memset", epsb[:], EPS), writes=[Beps])
        qgs = k.sb("qgs", [128, 1]); Bqgs = Buf("qgs")
        P.op("vector", _m("tensor_scalar", out=qgs[:], in0=qkgt[:, 0:1], scalar1=float(ATT_SCALE), scalar2=None, op0=ALU.mult), reads=[Bqkg], writes=[Bqgs])
        pmisc = Ring(k, "pmisc", 2, [128, 512], F32, psum=True)
        pscr = Ring(k, "pscr", 3, [128, 512], F32, psum=True)
        po_r = Ring(k, "po", 2, [128, 512], F32, psum=True); pden = k.ps("pden"); Bpden = Buf("pden")
        dacc_r = Ring(k, "dacc", 4, [128, 512], F32)
        x_r = Ring(k, "xin", 2, [128, 512], F32)
        cs_r = Ring(k, "cst", 2, [128, 2, 512], F32)
        sq_r = Ring(k, "sq", 2, [128, 512], F32)
        rs_r = Ring(k, "rs", 2, [128, 512], F32)
        xn_r = Ring(k, "xn", 2, [128, 512], F32)
        t_r = Ring(k, "tt", 3, [128, 512], F32)
        qb_r = Ring(k, "qb", 2, [128, 512], BF16)
        pT_r = Ring(k, "pT", 5, [128, 512], BF16)
        ob_r = Ring(k, "ob", 2, [128, 512], BF16)
        rd_r = Ring(k, "rd", 1, [128, 512], F32)
        kb = [k.sb("kb%d" % i, [128, NTOT], BF16) for i in range(NKV)]; Bkb = [Buf("kb%d" % i) for i in range(NKV)]
        vtm = [k.sb("vtm%d" % i, [128, NKC, 128], BF16) for i in range(NKV)]; Bvtm = [Buf("vtm%d" % i) for i in range(NKV)]

        def normrot(src_ap, Bsrc, W, gcol, Bg, pos0, dst_ap, Bdst, cst, Bcs):
            sq, Bsq = sq_r.next()
            P.op("scalar", _m("activation", out=sq[:, :W], in_=src_ap, func=AF.Square), reads=[Bsrc], writes=[Bsq])
            ps, Bps = pmisc.next()
            P.op("tensor", _m("matmul", ps[:, :W], lhsT=ones32[:], rhs=sq[:, :W], start=True, stop=True), reads=[Bsq, Bo32], writes=[Bps])
            rs, Brs = rs_r.next()
            P.op("scalar", _m("activation", out=rs[:, :W], in_=ps[:, :W], func=AF.Sqrt, bias=epsb[:, 0:1], scale=1.0 / 128), reads=[Bps, Beps], writes=[Brs])
            P.op("vector", _m("reciprocal", out=rs[:, :W], in_=rs[:, :W]), reads=[Brs], writes=[Brs])
            if pos0 is None:
                P.op("vector", _m("scalar_tensor_tensor", out=dst_ap, in0=src_ap, scalar=gcol, in1=rs[:, :W], op0=ALU.mult, op1=ALU.mult),
                     reads=[Bsrc, Bg, Brs], writes=[Bdst])
                return
            xn, Bxn = xn_r.next()
            P.op("vector", _m("scalar_tensor_tensor", out=xn[:, :W], in0=src_ap, scalar=gcol, in1=rs[:, :W], op0=ALU.mult, op1=ALU.mult),
                 reads=[Bsrc, Bg, Brs], writes=[Bxn])
            pp, Bpp = pmisc.next()
            P.op("tensor", _m("matmul", pp[:, :W], lhsT=pmt[:], rhs=xn[:, :W], start=True, stop=True), reads=[Bxn, Bpm], writes=[Bpp])
            t1, Bt1 = t_r.next(); t2, Bt2 = t_r.next()
            P.op("gpsimd", _m("tensor_tensor", out=t1[:, :W], in0=xn[:, :W], in1=cst[:, 0, :W], op=ALU.mult), reads=[Bxn, Bcs], writes=[Bt1])
            P.op("vector", _m("tensor_tensor", out=t2[:, :W], in0=pp[:, :W], in1=cst[:, 1, :W], op=ALU.mult), reads=[Bpp, Bcs], writes=[Bt2])
            P.op("vector", _m("tensor_tensor", out=dst_ap, in0=t1[:, :W], in1=t2[:, :W], op=ALU.add), reads=[Bt1, Bt2], writes=[Bdst])

        def load_cs(pos0, W, own=False):
            cst, Bcs = cs_r.next()
            P.dma("scalar", cst[:, 0, :W], (cosQ if own else cosF)[:, pos0:pos0 + W], writes=[Bcs], sem=Bcs)
            P.dma("scalar", cst[:, 1, :W], (sinQ if own else sinS)[:, pos0:pos0 + W], writes=[Bcs], sem=Bcs)
            return cst, Bcs

        PW = 512
        xfull = k.sb("xfull", [128, NLAT + 16], F32); Bxf = Buf("xfull")
        s_r = [Ring(k, "ps%d_" % i, 1, [128, PW + 16], F32) for i in range(2)]
        ic_r = Ring(k, "ic", 1, [128, PW], F32)
        df_r = Ring(k, "df", 1, [128, PW], BF16)
        for gl in range(4):
            w = POOL_WINDOWS[gl]
            P.dma("sync", xfull[:], (lambda h1, gl=gl: poolS[gl][bass.ds(h1, 1), :, :].rearrange("o p t -> p (o t)")),
                  reads=[BpoolS], writes=[Bxf], sem=Bxf)
            for t0 in range(0, NLAT, PW):
                xp, Bxp = xfull[:, t0:t0 + PW + 16], Bxf
                ic, Bic = ic_r.next()
                P.dma("scalar", ic[:], icnt[:, gl, t0:t0 + PW], writes=[Bic], sem=Bic)
                cur, Bcur, span, n = xp, Bxp, 1, PW + 16
                i = 0
                while span < w:
                    nxt, Bnxt = s_r[i % 2].next(); i += 1
                    n2 = n - span
                    P.op("vector", _m("tensor_tensor", out=nxt[:, :n2], in0=cur[:, 0:n2], in1=cur[:, span:span + n2], op=ALU.add), reads=[Bcur], writes=[Bnxt])
                    cur, Bcur, span, n = nxt, Bnxt, span * 2, n2
                off = 8 - w // 2
                P.op("gpsimd", _m("tensor_tensor", out=ic[:], in0=cur[:, off:off + PW], in1=ic[:], op=ALU.mult), reads=[Bcur, Bic], writes=[Bic])
                df, Bdf = df_r.next()
                P.op("vector", _m("tensor_tensor", out=df[:], in0=ic[:], in1=xp[:, 8:8 + PW], op=ALU.subtract), reads=[Bic, Bxp], writes=[Bdf])
                for s in range(0, PW, 512):
                    ps, Bps = pmisc.next()
                    P.op("tensor", _m("matmul", ps[:], lhsT=pwt[:, gl, :], rhs=df[:, s:s + 512], start=True, stop=True), reads=[Bpw, Bdf], writes=[Bps])
                    ob, Bob_ = ob_r.next()
                    P.op("scalar", _m("activation", out=ob[:], in_=ps[:], func=AF.Copy, scale=1.0) if False else
                         _m("activation", out=ob[:], in_=ps[:], func=AF.Identity, scale=pst_[:, gl:gl + 1], bias=0.0), reads=[Bps, Bpsc], writes=[Bob_])
                    P.dma("sync", mixB[gl * 128:(gl + 1) * 128, t0 + s:t0 + s + 512], ob[:], reads=[Bob_], writes=[BmixB], sem=Bob_)

        ktiles = [(0, LC, None)] + [(LC + i * 512, 512, i * 512) for i in range(S_ // 512)]
        for kv in range(NKV):
            for (c0, W, pos0) in ktiles:
                x, Bx_ = x_r.next()
                P.dma("sync", x[:, :W], kvS[kv * 128:(kv + 1) * 128, c0:c0 + W], reads=[BkvS], writes=[Bx_], sem=Bx_)
                cst, Bcs = load_cs(pos0, W) if pos0 is not None else (None, None)
                normrot(x[:, :W], Bx_, W, qkgt[:, 1:2], Bqkg, pos0, kb[kv][:, c0:c0 + W], Bkb[kv], cst, Bcs)
                v, Bv = x_r.next()
                P.dma("sync", v[:, :W], kvS[512 + kv * 128:512 + (kv + 1) * 128, c0:c0 + W], reads=[BkvS], writes=[Bv], sem=Bv)
                pt, Bpt = pmisc.next()
                for c in range(W // 128):
                    P.op("tensor", _m("transpose", out=pt[:, c * 128:(c + 1) * 128], in_=v[:, c * 128:(c + 1) * 128], identity=idt[:]), reads=[Bv, Bid], writes=[Bpt])
                P.op("scalar", _m("activation", out=vtm[kv][:, c0 // 128:(c0 + W) // 128, :].rearrange("p a b -> p (a b)"), in_=pt[:, :W], func=AF.Copy),
                     reads=[Bpt], writes=[Bvtm[kv]])

        cs_cache = {}

        def prep_q(qt, h):
            pos0 = qt * 512
            if qt not in cs_cache:
                cs_cache.clear()
                cs_cache[qt] = load_cs(pos0, 512, own=True)
            cst, Bcs = cs_cache[qt]
            x, Bx_ = x_r.next()
            P.dma("sync", x[:], qS[h * 128:(h + 1) * 128, pos0:pos0 + 512], reads=[BqS], writes=[Bx_], sem=Bx_)
            qb, Bqb = qb_r.next()
            normrot(x[:], Bx_, 512, qgs[:, 0:1], Bqgs, pos0, qb[:], Bqb, cst, Bcs)
            return qb, Bqb

        jobs = [(qt, h) for qt in range(NQT) for h in range(NQH)]
        nxt = prep_q(*jobs[0])
        for ji, (qt, h) in enumerate(jobs):
            if True:
                pos0 = qt * 512
                kv = h // 3
                qb, Bqb = nxt
                if ji + 1 < len(jobs):
                    nxt = prep_q(*jobs[ji + 1])
                LAG = 2
                pend = []
                po, Bpo = po_r.next()
                dacc = [dacc_r.next(), dacc_r.next()]
                for sc in range(NKC + LAG):
                    if sc < NKC:
                        ps, Bps = pscr.next()
                        P.op("tensor", _m("matmul", ps[:], lhsT=kb[kv][:, sc * 128:(sc + 1) * 128], rhs=qb[:], start=True, stop=True), reads=[Bkb[kv], Bqb], writes=[Bps])
                        pT, BpT = pT_r.next()
                        P.op("scalar", _m("activation", out=pT[:], in_=ps[:], func=AF.Exp), reads=[Bps], writes=[BpT])
                        pend.append((sc, pT, BpT))
                    if sc >= LAG:
                        s2, pT2, BpT2 = pend.pop(0)
                        P.op("tensor", _m("matmul", po[:], lhsT=vtm[kv][:, s2, :], rhs=pT2[:], start=(s2 == 0), stop=(s2 == NKC - 1)), reads=[Bvtm[kv], BpT2], writes=[Bpo])
                        da, Bda = dacc[s2 % 2]
                        if s2 < 2:
                            P.op("vector", _m("tensor_copy", out=da[:], in_=pT2[:]), reads=[BpT2], writes=[Bda])
                        else:
                            P.op("vector", _m("tensor_tensor", out=da[:], in0=da[:], in1=pT2[:], op=ALU.add), reads=[Bda, BpT2], writes=[Bda])
                for i2 in range(2):
                    P.op("tensor", _m("matmul", pden[:], lhsT=ones32[:], rhs=dacc[i2][0][:], start=(i2 == 0), stop=(i2 == 1)), reads=[Bo32, dacc[i2][1]], writes=[Bpden])
                rd, Brd = rd_r.next()
                P.op("vector", _m("reciprocal", out=rd[:], in_=pden[:]), reads=[Bpden], writes=[Brd])
                of, Bof = t_r.next()
                P.op("scalar", _m("activation", out=of[:], in_=po[:], func=AF.Copy), reads=[Bpo], writes=[Bof])
                ob, Bob_ = ob_r.next()
                P.op("gpsimd", _m("tensor_tensor", out=ob[:], in0=of[:], in1=rd[:], op=ALU.mult), reads=[Bof, Brd], writes=[Bob_])
                P.dma("sync", mixB[512 + h * 128:512 + (h + 1) * 128, pos0:pos0 + 512], ob[:], reads=[Bob_], writes=[BmixB], sem=Bob_)


def prep_mod_into(R, modv, Bmodv, npre, Bnpre, npost, Bnpost, nkinds, subs, wsteps):
    P = R.k.P
    R.Bmod = Buf("modABC1")
    for kd in range(nkinds):
        for sub in subs:
            sh = modv[:, sub * 48 + 0:sub * 48 + 16, kd]
            sc = modv[:, sub * 48 + 16:sub * 48 + 32, kd]
            gt = modv[:, sub * 48 + 32:sub * 48 + 48, kd]
            P.op("vector", _m("scalar_tensor_tensor", out=R.A[:, kd, sub, :], in0=sc, scalar=1.0, in1=npre[:, sub, :], op0=ALU.add, op1=ALU.mult),
                 reads=[Bmodv, Bnpre], writes=[R.Bmod])
            P.op("vector", _m("tensor_copy", out=R.Bv[:, kd, sub, :], in_=sh), reads=[Bmodv], writes=[R.Bmod])
            P.op("vector", _m("scalar_tensor_tensor", out=R.C[:, kd, sub, :], in0=gt, scalar=float(wsteps[sub]), in1=npost[:, sub, :], op0=ALU.mult, op1=ALU.mult),
                 reads=[Bmodv, Bnpost], writes=[R.Bmod])


@contextlib.contextmanager
def phase(nc, tag):
    with nc.cleanup_on_exit():
        k = K(nc, tag)
        with k.es:
            yield k
            k.P.barrier()
            k.P.build()


def emit_mod(k, t):
    P = k.P
    cT, mw, mb, modS = t["cT"], t["mw"], t["mb"], t["modS"]
    NCH = 288
    ct = k.sb("ct", [128, 16, 8]); Bct = Buf("ct")
    sct = k.sb("sct", [128, 16, 8], BF16); Bsct = Buf("sct")
    mbt = k.sb("mbt", [128, NCH]); Bmb = Buf("mb")
    res = k.sb("res", [128, NCH, 8]); Bres = Buf("res")
    wr = Ring(k, "mw", 6, [128, 16, 128], BF16)
    pr = Ring(k, "pm", 4, [128, 512], F32, psum=True)
    P.dma("sync", ct[:], cT, writes=[Bct], sem=Bct)
    P.dma("sync", mbt[:], mb, writes=[Bmb], sem=Bmb)
    P.op("scalar", _m("activation", out=sct[:], in_=ct[:], func=AF.Silu), reads=[Bct], writes=[Bsct])
    for j in range(NCH):
        wt, Bw = wr.next()
        P.dma("gpsimd", wt[:], mw[j], writes=[Bw], sem=Bw, max_dma_last_dim=8192)
        ps, Bps = pr.next()
        for kc in range(16):
            P.op("tensor", _m("matmul", ps[:, 0:8], lhsT=wt[:, kc, :], rhs=sct[:, kc, :], start=(kc == 0), stop=(kc == 15)),
                 reads=[Bw, Bsct], writes=[Bps])
        P.op("scalar", _m("activation", out=res[:, j, :], in_=ps[:, 0:8], func=AF.Identity, bias=mbt[:, j:j + 1], scale=1.0),
             reads=[Bps, Bmb], writes=[Bres])
    P.dma("sync", modS, res[:], reads=[Bres], sem=Bres)


def rl_setup(k, t, layers):
    R = RowLocal(k, TT); R.alloc_ffn(); R.init_eps()
    sets = {}
    for l in layers:
        cs = load_consts(k, [("modv%d" % l, [128, 144, 2]), ("npre%d" % l, [128, 3, 16]), ("npost%d" % l, [128, 3, 16])],
                         {"modv%d" % l: t["modS"][:, l * 144:(l + 1) * 144, 0:2], "npre%d" % l: t["npre"][l], "npost%d" % l: t["npost"][l]})
        R.A = k.sb("modA%d" % l, [128, 2, 3, 16], F32); R.Bv = k.sb("modB%d" % l, [128, 2, 3, 16], F32); R.C = k.sb("modC%d" % l, [128, 2, 3, 16], F32)
        prep_mod_into(R, cs["modv%d" % l][0], cs["modv%d" % l][1], cs["npre%d" % l][0], cs["npre%d" % l][1],
                      cs["npost%d" % l][0], cs["npost%d" % l][1], 2, [0, 1, 2], [0.5, 1.0, 0.5])
        sets[l] = (R.A, R.Bv, R.C, R.Bmod)

    def use(l):
        R.A, R.Bv, R.C, R.Bmod = sets[l]
    return R, use


def all_tiles():
    return [(0, LC, 1)] + [(LC + i * TT, TT, 0) for i in range(S_ // TT)]


def emit_A(k, t):
    R, use = rl_setup(k, t, [0]); use(0)
    for (c0, T, kd) in all_tiles():
        R.load_x(t["xT"], c0, T)
        R.ffn_sublayer(kd, 0, T, t["wgu"][0][0], t["wdn"][0][0])
        R.store_x(t["x1S"], c0, T)
        R.proj_out(kd, 1, T, t["win_ev"], 48, t["proj"], c0)


def emit_C(k, t):
    P = k.P
    R, use = rl_setup(k, t, [0, 1])
    Bx4 = Buf("x4S")
    poolS = t["poolS"]
    zt = k.sb("zt", [128, 8]); Bz = Buf("zt")
    P.op("gpsimd", _m("memset", zt[:], 0.0), writes=[Bz])
    for g in range(4):
        P.dma("sync", poolS[g][0, :, 0:8], zt[:], reads=[Bz], sem=Bz)
        P.dma("sync", poolS[g][1, :, NLAT + 8:NLAT + 16], zt[:], reads=[Bz], sem=Bz)

    def route_for(c0, T):
        def route(n):
            if n >= 4:
                return [(t["kvS"][(n - 4) * 128:(n - 3) * 128, c0:c0 + T], slice(0, T))]
            if c0 < LC:
                return []
            a = c0 - LC
            h = a // NLAT
            loc = a - h * NLAT
            outs = [(poolS[n][h, :, 8 + loc:8 + loc + T], slice(0, T))]
            if h == 1 and loc == 0:
                outs.append((poolS[n][0, :, 8 + NLAT:16 + NLAT], slice(0, 8)))
            if h == 0 and loc + T == NLAT:
                outs.append((poolS[n][1, :, 0:8], slice(T - 8, T)))
            return outs
        return route

    for ti, (c0, T, kd) in enumerate(all_tiles()):
        R.load_x(t["x1S"], c0, T)
        R.load_h(t["mixA"], c0, T)
        use(0)
        R.mixer_out_sublayer(kd, 1, T, t["wout_ev"])
        R.ffn_sublayer(kd, 2, T, t["wgu"][0][1], t["wdn"][0][1])
        use(1)
        R.ffn_sublayer(kd, 0, T, t["wgu"][1][0], t["wdn"][1][0])
        if c0 >= LC:
            a = c0 - LC
            R.store_x(t["x4S"][(a % NLAT) // TT][a // NLAT], 0, T, Bdst=Bx4)
        R.proj_out(kd, 1, T, t["win_pkv"], 12, None, c0, route=route_for(c0, T))
    use(1)
    for j in range(NLAT // TT):
        R.load_x(t["x4S"], 0, TT, dyn=j, Bsrc=Bx4)
        R.proj_out(0, 1, TT, t["win_q"], 12, t["qS"], j * TT)


def emit_E(k, t):
    R, use = rl_setup(k, t, [1]); use(1)
    for j in range(NLAT // TT):
        R.load_x(t["x4S"], 0, TT, dyn=j)
        R.load_h(t["mixB"], j * TT, TT)
        R.mixer_out_sublayer(0, 1, TT, t["wout_od"])
        R.ffn_sublayer(0, 2, TT, t["wgu"][1][1], t["wdn"][1][1])
        R.store_x(t["outT"], j * TT, TT)


def build_fused():
    nc = bass.Bass("TRN2", target_bir_lowering=False)

    def din(name, shape, dt=F32):
        return nc.dram_tensor(name, list(shape), dt, kind="ExternalInput").ap()

    def scr(name, shape, dt=F32):
        return nc.dram_tensor(name, list(shape), dt).ap()

    t = {}
    t["xT"] = din("xT", [D, NTOT]); t["cT"] = din("cT", [128, 16, 8]); t["mw"] = din("mw", [288, 128, 16, 128]); t["mb"] = din("mb", [128, 288])
    t["npre"] = din("npre", [2, 128, 3, 16]); t["npost"] = din("npost", [2, 128, 3, 16])
    t["wgu"] = [[din("wgu%d%d" % (l, j), [2 * NFC, 128, NDC, 128]) for j in range(2)] for l in range(2)]
    t["wdn"] = [[din("wdn%d%d" % (l, j), [NDC, 128, NFC, 128]) for j in range(2)] for l in range(2)]
    t["win_ev"] = din("win_ev", [48, 128, NDC, 128]); t["wout_ev"] = din("wout_ev", [NDC, 128, NDC, 128])
    t["win_pkv"] = din("win_pkv", [12, 128, NDC, 128]); t["win_q"] = din("win_q", [12, 128, NDC, 128]); t["wout_od"] = din("wout_od", [NDC, 128, NDC, 128])
    t["cw"] = din("cw", [128, 8, 5]); t["wab"] = din("wab", [128, 2, 2, 8, 128]); t["bab"] = din("bab", [128, 2, 2, 8]); t["lam"] = din("lam", [128, 2, 8])
    t["cos1T"] = din("cos1T", [128, S_]); t["sin1T"] = din("sin1T", [128, S_])
    t["dmat"] = din("dmat", [128, 4, 128]); t["iq"] = din("iq", [128, 2, 128]); t["jk"] = din("jk", [128, 2]); t["ident"] = din("ident", [128, 128])
    t["dlb"] = din("dlb", [128, 2, 4]); t["gng"] = din("gng", [128, 4, 2])
    t["cosF"] = din("cosF", [128, S_]); t["sinS"] = din("sinS", [128, S_]); t["pm"] = din("pm", [128, 128])
    t["cosQ"] = din("cosQ", [128, NLAT]); t["sinQ"] = din("sinQ", [128, NLAT])
    t["icnt"] = din("icnt", [128, 4, NLAT]); t["hmask"] = din("hmask", [128, 16])
    t["pw"] = din("pw", [128, 4, 128]); t["pscale"] = din("pscale", [128, 4]); t["qkg"] = din("qkg", [128, 2])
    t["outT"] = nc.dram_tensor("outT", [D, NLAT], F32, kind="ExternalOutput").ap()
    t["modS"] = scr("modS", [128, 288, 8]); t["x1S"] = scr("x1S", [D, NTOT]); t["proj"] = scr("proj", [6144, NTOT])
    t["h0s"] = scr("h0s", [1024, NTOT]); t["o0s"] = scr("o0s", [512, NTOT]); t["mixA"] = scr("mixA", [D, NTOT], BF16)
    t["x4S"] = [scr("x4S%d" % j, [2, D, TT]) for j in range(NLAT // TT)]; t["poolS"] = [scr("poolS%d" % g, [2, 128, NLAT + 16]) for g in range(4)]; t["kvS"] = scr("kvS", [1024, NTOT])
    t["qS"] = scr("qS", [1536, NLAT]); t["mixB"] = scr("mixB", [D, NLAT], BF16)
    for nm in ("proj", "mixA", "poolS", "kvS", "qS", "mixB"):
        t["B" + nm] = Buf(nm)

    with phase(nc, "m_") as k:
        emit_mod(k, t)
    with phase(nc, "a_") as k:
        emit_A(k, t)
    with phase(nc, "l_") as k:
        for nm in ("proj", "mixA"):
            t["B" + nm] = Buf(nm)
        emit_B1(k, t, 8)
    for hp in range(2):
        with phase(nc, "r%d_" % hp) as k:
            for nm in ("proj", "mixA"):
                t["B" + nm] = Buf(nm)
            emit_B2(k, t, hp)
    with phase(nc, "c_") as k:
        emit_C(k, t)
    with phase(nc, "d_") as k:
        for nm in ("poolS", "kvS", "qS", "mixB"):
            t["B" + nm] = Buf(nm)
        emit_D(k, t)
    with phase(nc, "e_") as k:
        emit_E(k, t)
    return nc


def host_B1_full(lru_conv_w, lru_conv_b, lru_wa, lru_ba, lru_wx, lru_bx, lru_lambda):
    cwv = np.concatenate([lru_conv_w[0], lru_conv_b[0][None]], 0)
    cw = np.ascontiguousarray(cwv.reshape(5, 8, 128).transpose(2, 1, 0))
    wab = np.stack([lru_wa[0], lru_wx[0]], 0)
    wab = np.ascontiguousarray(wab.transpose(3, 0, 1, 2, 4))
    bab = np.stack([lru_ba[0], lru_bx[0]], 0).reshape(2, 2, 8, 128)
    bab = np.ascontiguousarray(bab.transpose(3, 0, 1, 2))
    lam = np.ascontiguousarray(lru_lambda[0].reshape(2, 8, 128).transpose(2, 0, 1))
    return {"cw": cw.astype(np.float32), "wab": wab.astype(np.float32), "bab": bab.astype(np.float32), "lam": lam.astype(np.float32)}


_NC_CACHE = {}


def kernel(x, c, ctx, c_ctx, mod_w, mod_b, norm_pre, norm_post, ffn_gate, ffn_up, ffn_down,
           ev_w_in, ev_w_out, lru_conv_w, lru_conv_b, lru_wa, lru_ba, lru_wx, lru_bx, lru_lambda,
           ret_decay_logit, ret_gn, od_w_in, od_w_out, pool_w, pool_scale, q_norm, k_norm):
    f32 = lambda a: np.asarray(a, dtype=np.float32)
    x, c, ctx, c_ctx, mod_w, mod_b = map(f32, (x, c, ctx, c_ctx, mod_w, mod_b))
    norm_pre, norm_post, ffn_gate, ffn_up, ffn_down = map(f32, (norm_pre, norm_post, ffn_gate, ffn_up, ffn_down))
    ev_w_in, ev_w_out, od_w_in, od_w_out = map(f32, (ev_w_in, ev_w_out, od_w_in, od_w_out))
    lru_conv_w, lru_conv_b, lru_wa, lru_ba, lru_wx, lru_bx, lru_lambda = map(
        f32, (lru_conv_w, lru_conv_b, lru_wa, lru_ba, lru_wx, lru_bx, lru_lambda))
    ret_decay_logit, ret_gn, pool_w, pool_scale, q_norm, k_norm = map(f32, (ret_decay_logit, ret_gn, pool_w, pool_scale, q_norm, k_norm))

    shared = {}
    shared["mw"] = np.ascontiguousarray(np.concatenate([mod_w[l].reshape(16, 128, 144, 128).transpose(2, 1, 0, 3) for l in range(2)], 0))
    shared["mb"] = np.ascontiguousarray(np.concatenate([mod_b[l].reshape(144, 128).T for l in range(2)], 1))
    shared["npre"] = np.stack([vec16(norm_pre[l]) for l in range(2)], 0); shared["npost"] = np.stack([vec16(norm_post[l]) for l in range(2)], 0)
    for l in range(2):
        for j in range(2):
            shared["wgu%d%d" % (l, j)] = tile_gu(ffn_gate[l, j], ffn_up[l, j])
            shared["wdn%d%d" % (l, j)] = tile_w(ffn_down[l, j], NFC, NDC)
    shared["win_ev"] = tile_w(ev_w_in[0], NDC, 48); shared["wout_ev"] = tile_w(ev_w_out[0], NDC, NDC)
    od = od_w_in[0]
    shared["win_pkv"] = tile_w(np.concatenate([od[:, 0:512], od[:, 2048:3072]], 1), NDC, 12)
    shared["win_q"] = tile_w(od[:, 512:2048], NDC, 12)
    shared["wout_od"] = tile_w(od_w_out[0], NDC, NDC)
    shared.update(host_B1_full(lru_conv_w, lru_conv_b, lru_wa, lru_ba, lru_wx, lru_bx, lru_lambda))
    shared["cos1T"], shared["sin1T"] = host_rot1_tables()
    shared.update(host_ret_consts())
    shared["dlb"] = np.ascontiguousarray(np.broadcast_to(ret_decay_logit[0][None], (128, 2, 4))).astype(np.float32)
    shared["gng"] = np.ascontiguousarray(ret_gn[0].reshape(4, 2, 128).transpose(2, 0, 1)).astype(np.float32)
    cosF, sinS, pm = host_rot2_tables()
    shared["cosF"], shared["sinS"], shared["pm"] = cosF, sinS, pm
    shared["pw"] = np.ascontiguousarray(pool_w[0].transpose(1, 0, 2)); shared["pscale"] = np.ascontiguousarray(pool_scale[0].reshape(4, 128).T)
    shared["qkg"] = np.ascontiguousarray(np.stack([q_norm[0], k_norm[0]], 1))

    maps = []
    for core in range(8):
        b, half = core // 2, core % 2
        m = dict(shared)
        m["xT"] = np.ascontiguousarray(np.concatenate([ctx[b], x[b]], 0).T)
        cc = np.zeros((8, D), np.float32); cc[0] = c[b]; cc[1] = c_ctx
        m["cT"] = np.ascontiguousarray(cc.T.reshape(16, 128, 8).transpose(1, 0, 2))
        m["cosQ"] = np.ascontiguousarray(cosF[:, half * NLAT:(half + 1) * NLAT]); m["sinQ"] = np.ascontiguousarray(sinS[:, half * NLAT:(half + 1) * NLAT])
        m["icnt"] = host_pool_icnt(half)
        hm = np.ones((128, 16), np.float32)
        if half == 0:
            hm[:, 0:8] = 0.0
        else:
            hm[:, 8:16] = 0.0
        m["hmask"] = hm
        maps.append(m)
    if "f" not in _NC_CACHE:
        _NC_CACHE["f"] = build_fused()
    res = run_bass_kernel_spmd(_NC_CACHE["f"], maps, core_ids=list(range(8))).results
    out = np.empty((B_, S_, D), np.float32)
    for core in range(8):
        b, half = core // 2, core % 2
        out[b, half * NLAT:(half + 1) * NLAT, :] = res[core]["outT"].T
    return out
```

```python
import contextlib
import numpy as np
import concourse.bass as bass
import concourse.mybir as mybir
from concourse.bass_utils import run_bass_kernel_spmd

F32 = mybir.dt.float32
BF16 = mybir.dt.bfloat16
AF = mybir.ActivationFunctionType
ALU = mybir.AluOpType
AX = mybir.AxisListType

ENGS = ("tensor", "vector", "scalar", "gpsimd", "sync")
EPOCH = 30000
DYN_STRIDE = 4096


def _m(name, *args, **kwargs):
    def call(e):
        return getattr(e, name)(*args, **kwargs)
    return call


class Buf:
    __slots__ = ("name", "lw", "rd", "sem", "cum")

    def __init__(self, name=""):
        self.name = name
        self.lw = None
        self.rd = []
        self.sem = None
        self.cum = 0


class Prog:
    def __init__(self, nc, same_engine_sync=True):
        self.nc = nc
        self.q = {e: [] for e in ENGS}
        self.cnt = {e: 0 for e in ENGS}
        self.waited = {e: {} for e in ENGS}
        self.semkeys = {}
        self.same = same_engine_sync
        self.ndma = 0
        self.dma_cum = {}

    def _signal_last(self, eng):
        q = self.q[eng]
        for ent in reversed(q):
            if ent[0] == "op":
                if ent[2] is None:
                    self.cnt[eng] += 1
                    c = self.cnt[eng]
                    key = ("e", eng, (c - 1) // EPOCH)
                    self.semkeys.setdefault(key, None)
                    ent[2] = (key, (c - 1) % EPOCH + 1)
                return ent[2]
        return None

    def _resolve(self, tok):
        if tok[0] == "lazy":
            ent = tok[2]
            if ent[2] is not None:
                return ent[2]
            eng = tok[1]
            t = self._signal_last(eng)
            ent[3] = t
            return t
        return tok

    def _wait(self, eng, tok):
        if tok is None:
            return
        if tok[0] == "lazy":
            src_eng = tok[1]
            if src_eng == eng and (eng == "tensor" or not self.same):
                return
            ent = tok[2]
            if ent[2] is None and ent[3] is not None:
                ctok = ent[3]
            else:
                ctok = self._resolve(tok)
        else:
            ctok = tok
        key, val = ctok
        w = self.waited[eng]
        if w.get(key, 0) >= val:
            return
        w[key] = val
        self.q[eng].append(["wait", key, val])

    def op(self, eng, fn, reads=(), writes=(), sig=False):
        for b in reads:
            self._wait(eng, b.lw)
        for b in writes:
            self._wait(eng, b.lw)
            for t in b.rd:
                self._wait(eng, t)
        ent = ["op", fn, None, None]
        self.q[eng].append(ent)
        if eng != "tensor" or sig:
            self.cnt[eng] += 1
            c = self.cnt[eng]
            key = ("e", eng, (c - 1) // EPOCH)
            self.semkeys.setdefault(key, None)
            ent[2] = (key, (c - 1) % EPOCH + 1)
        tok = ("lazy", eng, ent)
        for b in writes:
            b.lw = tok
            b.rd = []
        for b in reads:
            b.rd.append(tok)
            if len(b.rd) > 6:
                last = {}
                keep = []
                for t in b.rd:
                    if t[0] == "lazy":
                        last[t[1]] = t
                    else:
                        keep.append(t)
                b.rd = keep[-4:] + list(last.values())
        return tok

    def dma(self, eng, out, in_, reads=(), writes=(), sem=None, **kw):
        for b in reads:
            self._wait(eng, b.lw)
        sb = sem
        if sb.sem is None:
            sb.sem = ("d", self.ndma)
            self.ndma += 1
            self.semkeys[sb.sem] = None
        for b in writes:
            if not (b.lw is not None and b.lw[0] == sb.sem):
                self._wait(eng, b.lw)
            for t in b.rd:
                self._wait(eng, t)
        sb.cum += 16
        self.dma_cum[sb.sem] = sb.cum
        tok = (sb.sem, sb.cum)
        self.q[eng].append(["dmad" if (callable(out) or callable(in_)) else "dma", out, in_, kw, sb.sem])
        for b in writes:
            b.lw = tok
            b.rd = []
        for b in reads:
            b.rd.append(tok)
            if len(b.rd) > 8:
                b.rd = b.rd[-8:]
        return tok

    def cc(self, eng, fn, reads=(), writes=(), sem=None):
        sb = sem
        if sb.sem is None:
            sb.sem = ("d", self.ndma)
            self.ndma += 1
            self.semkeys[sb.sem] = None
        for b in reads:
            self._wait(eng, b.lw)
        for b in writes:
            self._wait(eng, b.lw)
            for t in b.rd:
                self._wait(eng, t)
        sb.cum += 16
        self.dma_cum[sb.sem] = sb.cum
        tok = (sb.sem, sb.cum)
        self.q[eng].append(["cc", fn, sb.sem])
        for b in writes:
            b.lw = tok
            b.rd = []
        for b in reads:
            b.rd.append(tok)
        return tok

    def wait_all(self, eng, bufs):
        for b in bufs:
            self._wait(eng, b.lw)
            for t in b.rd:
                self._wait(eng, t)

    def barrier(self):
        toks = []
        for e in ENGS:
            t = self._signal_last(e)
            if t is not None:
                toks.append(t)
        for key in list(self.semkeys):
            if key[0] == "d":
                toks.append((key, self.dma_cum[key]))
        for e in ENGS:
            for t in toks:
                self._wait(e, t)

    _uid = 0

    def build(self):
        nc = self.nc
        Prog._uid += 1
        for key in self.semkeys:
            self.semkeys[key] = nc.alloc_semaphore(name="s%d_" % Prog._uid + "_".join(str(k) for k in key))
        sems = self.semkeys

        def replay(eng_name, e):
            dyn = {}
            for ent in self.q[eng_name]:
                k = ent[0]
                if k == "op":
                    ins = ent[1](e)
                    if ent[2] is not None:
                        ins.then_inc(sems[ent[2][0]], 1)
                elif k == "wait":
                    e.wait_ge(sems[ent[1]], ent[2])
                elif k == "dmad":
                    if "hv" not in dyn:
                        dyn["hv"] = e.partition_id() & 1
                    o_ = ent[1](dyn["hv"]) if callable(ent[1]) else ent[1]
                    i_ = ent[2](dyn["hv"]) if callable(ent[2]) else ent[2]
                    e.dma_start(out=o_, in_=i_, **ent[3]).then_inc(sems[ent[4]], 16)
                elif k == "cc":
                    ent[1](e).then_inc(sems[ent[2]], 16)
                else:
                    e.dma_start(out=ent[1], in_=ent[2], **ent[3]).then_inc(sems[ent[4]], 16)

        with nc.Block() as block:
            @block.tensor
            def _(e):
                replay("tensor", e)

            @block.vector
            def _(e):
                replay("vector", e)

            @block.scalar
            def _(e):
                replay("scalar", e)

            @block.gpsimd
            def _(e):
                replay("gpsimd", e)

            @block.sync
            def _(e):
                replay("sync", e)

    def stats(self):
        return {e: (sum(1 for x in self.q[e] if x[0] != "wait"), sum(1 for x in self.q[e] if x[0] == "wait")) for e in ENGS}
D = 2048
NDC = 16
DFF = 5632
NFC = 44
B_, S_, LC = 4, 8192, 256
NLAT = 4096
NCTX = 128
NT = NLAT + NCTX
TT = 512
EPS = 1e-6


class K:
    def __init__(self, nc=None, tag=""):
        self.nc = nc if nc is not None else bass.Bass("TRN2", target_bir_lowering=False)
        self.es = contextlib.ExitStack()
        self.P = Prog(self.nc)
        self.nps = 0
        self._n = 0
        self.tag = tag

    def dscr(self, name, shape, dt=F32):
        return self.nc.dram_tensor(name, list(shape), dt).ap()

    def din(self, name, shape, dt=F32):
        return self.nc.dram_tensor(name, list(shape), dt, kind="ExternalInput").ap()

    def dout(self, name, shape, dt=F32):
        return self.nc.dram_tensor(name, list(shape), dt, kind="ExternalOutput").ap()

    def sb(self, name, shape, dt=F32):
        return self.es.enter_context(self.nc.sbuf_tensor("sb_" + self.tag + name, list(shape), dt))

    def ps(self, name, shape=(128, 512), dt=F32):
        self.nps += 1
        return self.es.enter_context(self.nc.psum_tensor("ps_" + self.tag + name, list(shape), dt))

    def uid(self, p="b"):
        self._n += 1
        return "%s%d" % (p, self._n)


class Ring:
    def __init__(self, k, name, n, shape, dt, psum=False):
        self.slots = []
        for i in range(n):
            t = k.ps("%s%d" % (name, i), shape, dt) if psum else k.sb("%s%d" % (name, i), shape, dt)
            self.slots.append((t, Buf("%s%d" % (name, i))))
        self.i = 0

    def next(self):
        s = self.slots[self.i % len(self.slots)]
        self.i += 1
        return s


def load_consts(k, names_shapes, dram):
    out = {}
    for name, shape in names_shapes:
        t = k.sb("c_" + name, shape, F32)
        b = Buf(name)
        k.P.dma("sync", t[:], dram[name], writes=[b], sem=b)
        out[name] = (t, b)
    return out


class RowLocal:
    def __init__(self, k, T):
        self.k = k
        P = k.P
        self.T = T
        self.x = k.sb("x", [128, NDC, T], F32); self.Bx = Buf("x")
        self.h = k.sb("h", [128, NDC, T], BF16); self.Bh = Buf("h")
        self.y = k.sb("y", [128, NDC, T], F32); self.By = Buf("y")
        self.act = None
        self.w16 = Ring(k, "w16_", 5, [128, 16, 128], BF16)
        self.w44 = None
        self.pmm = Ring(k, "pmm", 5, [128, 512], F32, psum=True)
        self.pst = Ring(k, "pst", 2, [128, 512], F32, psum=True)
        self.sq = Ring(k, "sq", 2, [128, T], BF16)
        self.tmp = Ring(k, "tmp", 3, [128, T], F32)
        self.sg = Ring(k, "sg", 2, [128, T], F32)
        self.rstd = k.sb("rstd", [128, T], F32); self.Brstd = Buf("rstd")
        self.rstd2 = k.sb("rstd2", [128, T], F32); self.Brstd2 = Buf("rstd2")
        self.stage = Ring(k, "stg", 4, [128, T], F32)
        self.ones = k.sb("ones", [128, 128], BF16); self.Bones = Buf("ones")
        P.op("gpsimd", _m("memset", self.ones[:], 1.0), writes=[self.Bones])
        self.flip = 0

    def alloc_ffn(self):
        k = self.k
        self.act = k.sb("act", [128, NFC, self.T], BF16); self.Bact = Buf("act")
        self.w44 = Ring(k, "w44_", 3, [128, 44, 128], BF16)

    def prep_mod(self, modv, Bmodv, npre, Bnpre, npost, Bnpost, nkinds, subs, wsteps):
        k = self.k; P = k.P
        self.A = k.sb("modA", [128, 2, 3, 16], F32)
        self.Bv = k.sb("modB", [128, 2, 3, 16], F32)
        self.C = k.sb("modC", [128, 2, 3, 16], F32)
        prep_mod_into(self, modv, Bmodv, npre, Bnpre, npost, Bnpost, nkinds, subs, wsteps)

    def rstd_from_psum(self, ps, Bps, out, Bout, T):
        P = self.k.P
        P.op("scalar", _m("activation", out=out[:, :T], in_=ps[:, :T], func=AF.Sqrt, bias=self.epsb[:, 0:1], scale=1.0 / D),
             reads=[Bps, self.Bepsb], writes=[Bout])
        P.op("vector", _m("reciprocal", out=out[:, :T], in_=out[:, :T]), reads=[Bout], writes=[Bout])

    def init_eps(self):
        k = self.k
        self.epsb = k.sb("epsb", [128, 1], F32); self.Bepsb = Buf("epsb")
        k.P.op("gpsimd", _m("memset", self.epsb[:], EPS), writes=[self.Bepsb])

    def norm_mod(self, kd, sub, T):
        k = self.k; P = k.P
        ps, Bps = self.pst.next()
        for dc in range(NDC):
            sq, Bsq = self.sq.next()
            P.op("scalar", _m("activation", out=sq[:, :T], in_=self.x[:, dc, :T], func=AF.Square),
                 reads=[self.Bx], writes=[Bsq])
            P.op("tensor", _m("matmul", ps[:, :T], lhsT=self.ones[:], rhs=sq[:, :T],
                                                                  start=(dc == 0), stop=(dc == NDC - 1)),
                 reads=[Bsq, self.Bones], writes=[Bps])
        self.rstd_from_psum(ps, Bps, self.rstd, self.Brstd, T)
        for dc in range(NDC):
            tmp, Bt = self.tmp.next()
            P.op("vector", _m("tensor_tensor", out=tmp[:, :T], in0=self.x[:, dc, :T], in1=self.rstd[:, :T], op=ALU.mult),
                 reads=[self.Bx, self.Brstd], writes=[Bt])
            P.op("scalar", _m("activation", out=self.h[:, dc, :T], in_=tmp[:, :T], func=AF.Identity,
                                                                 scale=self.A[:, kd, sub, dc:dc + 1], bias=self.Bv[:, kd, sub, dc:dc + 1]),
                 reads=[Bt, self.Bmod], writes=[self.Bh])

    def linear(self, rhs_of, Brhs, wdram, nk, nn, T, evac, ring=None):
        P = self.k.P
        ring = ring or (self.w16 if nk <= 16 else self.w44)
        pend = []
        LOOK = len(ring.slots) - 1

        def issue(n):
            wt, Bw = ring.next()
            P.dma("gpsimd", wt[:, :nk, :], wdram[n], writes=[Bw], sem=Bw, max_dma_last_dim=8192)
            return wt, Bw

        for n in range(min(LOOK, nn)):
            pend.append(issue(n))
        for n in range(nn):
            wt, Bw = pend.pop(0)
            ps, Bps = self.pmm.next()
            for kc in range(nk):
                P.op("tensor", _m("matmul", ps[:, :T], lhsT=wt[:, kc, :], rhs=rhs_of(kc),
                                                                      start=(kc == 0), stop=(kc == nk - 1)),
                     reads=[Bw, Brhs], writes=[Bps])
            if n + LOOK < nn:
                pend.append(issue(n + LOOK))
            evac(n, ps, Bps)

    def make_post_evac(self, T):
        P = self.k.P
        pst, Bpst = self.pst.next()
        state = {"prev": None}

        def flush_stat(last):
            if state["prev"] is None:
                return
            n, sq, Bsq = state["prev"]
            P.op("tensor", _m("matmul", pst[:, :T], lhsT=self.ones[:], rhs=sq[:, :T], start=(n == 0), stop=last),
                 reads=[Bsq, self.Bones], writes=[Bpst])
            state["prev"] = None

        def evac(n, ps, Bps):
            flush_stat(False)
            P.op("scalar", _m("activation", out=self.y[:, n, :T], in_=ps[:, :T], func=AF.Copy), reads=[Bps], writes=[self.By])
            sq, Bsq = self.sq.next()
            P.op("vector", _m("tensor_tensor", out=sq[:, :T], in0=self.y[:, n, :T], in1=self.y[:, n, :T], op=ALU.mult), reads=[self.By], writes=[Bsq])
            state["prev"] = (n, sq, Bsq)

        def finish(kd, sub):
            flush_stat(True)
            self.rstd_from_psum(pst, Bpst, self.rstd2, self.Brstd2, T)
            for dc in range(NDC):
                tmp, Bt = self.tmp.next()
                eng = "gpsimd" if dc % 4 == 3 else "vector"
                P.op(eng, _m("tensor_tensor", out=tmp[:, :T], in0=self.y[:, dc, :T], in1=self.rstd2[:, :T], op=ALU.mult),
                     reads=[self.By, self.Brstd2], writes=[Bt])
                P.op("vector", _m("scalar_tensor_tensor",
                    out=self.x[:, dc, :T], in0=tmp[:, :T], scalar=self.C[:, kd, sub, dc:dc + 1], in1=self.x[:, dc, :T],
                    op0=ALU.mult, op1=ALU.add), reads=[Bt, self.Bmod, self.Bx], writes=[self.Bx])
        return evac, finish

    def ffn_sublayer(self, kd, sub, T, wgu, wdn):
        P = self.k.P
        self.norm_mod(kd, sub, T)
        sgs = {}

        def evac_gu(n, ps, Bps):
            f = n // 2
            if n % 2 == 0:
                sg, Bsg = self.sg.next()
                P.op("scalar", _m("activation", out=sg[:, :T], in_=ps[:, :T], func=AF.Silu), reads=[Bps], writes=[Bsg])
                sgs[f] = (sg, Bsg)
            else:
                sg, Bsg = sgs.pop(f)
                P.op("vector", _m("tensor_tensor", out=self.act[:, f, :T], in0=sg[:, :T], in1=ps[:, :T], op=ALU.mult),
                     reads=[Bsg, Bps], writes=[self.Bact])

        self.linear(lambda kc: self.h[:, kc, :T], self.Bh, wgu, NDC, 2 * NFC, T, evac_gu)
        evac, finish = self.make_post_evac(T)
        self.linear(lambda kc: self.act[:, kc, :T], self.Bact, wdn, NFC, NDC, T, evac)
        finish(kd, sub)

    def mixer_out_sublayer(self, kd, sub, T, wout):
        evac, finish = self.make_post_evac(T)
        self.linear(lambda kc: self.h[:, kc, :T], self.Bh, wout, NDC, NDC, T, evac)
        finish(kd, sub)

    def proj_out(self, kd, sub, T, win, nn, proj_dram, t0, Bdst=None, route=None):
        P = self.k.P
        self.norm_mod(kd, sub, T)
        pj = proj_dram.rearrange("(c p) t -> c p t", p=128) if proj_dram is not None else None
        wr = [Bdst] if Bdst is not None else []

        def evac(n, ps, Bps):
            st, Bst = self.stage.next()
            if n % 2 == 0:
                P.op("scalar", _m("activation", out=st[:, :T], in_=ps[:, :T], func=AF.Copy), reads=[Bps], writes=[Bst])
            else:
                P.op("vector", _m("tensor_copy", out=st[:, :T], in_=ps[:, :T]), reads=[Bps], writes=[Bst])
            if route is None:
                P.dma("sync", pj[n, :, t0:t0 + T], st[:, :T], reads=[Bst], writes=wr, sem=Bst)
            else:
                for (dst, cols) in route(n):
                    P.dma("sync", dst, st[:, cols], reads=[Bst], writes=wr, sem=Bst)

        self.linear(lambda kc: self.h[:, kc, :T], self.Bh, win, NDC, nn, T, evac)

    def load_x(self, xdram, t0, T, dyn=None, Bsrc=None):
        P = self.k.P
        rd = [Bsrc] if Bsrc is not None else []
        if dyn is not None:
            in_ = lambda h1, j=dyn: xdram[j][bass.ds(h1, 1), :, :].rearrange("o (c p) t -> p (o c) t", p=128)
            P.dma("sync", self.x[:, :, :T], in_, reads=rd, writes=[self.Bx], sem=self.Bx)
            return
        src = xdram.rearrange("(c p) t -> p c t", p=128)
        for c4 in range(4):
            cs_ = slice(4 * c4, 4 * c4 + 4)
            P.dma("sync", self.x[:, cs_, :T], src[:, cs_, t0:t0 + T], reads=rd, writes=[self.Bx], sem=self.Bx)

    def store_x(self, xdram, t0, T, Bdst=None):
        P = self.k.P
        dst = xdram.rearrange("(c p) t -> p c t", p=128)
        wr = [Bdst] if Bdst is not None else []
        for c4 in range(4):
            P.dma("sync", dst[:, 4 * c4:4 * c4 + 4, t0:t0 + T], self.x[:, 4 * c4:4 * c4 + 4, :T], reads=[self.Bx], writes=wr, sem=self.Bx)

    def load_h(self, mdram, t0, T, Bsrc=None):
        P = self.k.P
        src = mdram.rearrange("(c p) t -> p c t", p=128)
        rd = [Bsrc] if Bsrc is not None else []
        for c4 in range(4):
            P.dma("scalar", self.h[:, 4 * c4:4 * c4 + 4, :T], src[:, 4 * c4:4 * c4 + 4, t0:t0 + T], reads=rd, writes=[self.Bh], sem=self.Bh)


def tiles_of(nlat, nctx, T):
    out = []
    t = 0
    while t < nlat:
        out.append((t, min(T, nlat - t), 0)); t += T
    t = nlat
    while t < nlat + nctx:
        out.append((t, min(T, nlat + nctx - t), 1)); t += T
    return out


def tile_w(w, nk, nn):
    return np.ascontiguousarray(w.reshape(nk, 128, nn, 128).transpose(2, 1, 0, 3))


def tile_gu(wg, wu):
    g = tile_w(wg, NDC, NFC); u = tile_w(wu, NDC, NFC)
    return np.ascontiguousarray(np.stack([g, u], axis=1).reshape(2 * NFC, 128, NDC, 128))


def vec16(v):
    v = np.asarray(v, np.float32)
    lead = v.shape[:-1]
    return np.ascontiguousarray(np.moveaxis(v.reshape(lead + (16, 128)), -1, 0))


NTOT = LC + S_
LW = 2048


def lru_tiles():
    ctx = [(0, LC, True, True)]
    lat = [(LC + i * LW, LW, i == 0, i == S_ // LW - 1) for i in range(S_ // LW)]
    return ctx, lat


def emit_B1(k, t, ncc=8):
    P = k.P
    proj, Bproj = t["proj"], t["Bproj"]
    cw, wab, bab, lam = t["cw"], t["wab"], t["bab"], t["lam"]
    out, Bout = t["mixA"], t["BmixA"]
    h0s = t["h0s"]
    Bh0s = Buf("h0s")
    if True:
        cs = load_consts(k, [("cw", [128, ncc, 5]), ("bab", [128, 2, 2, ncc]), ("lam", [128, 2, ncc])],
                         {"cw": cw, "bab": bab, "lam": lam})
        cwt, Bcw = cs["cw"]; babt, Bbab = cs["bab"]; lamt, Blam = cs["lam"]
        wt = k.sb("wab", [128, 2, 2, ncc, 128], BF16); Bwt = Buf("wab")
        P.dma("gpsimd", wt[:], wab, writes=[Bwt], sem=Bwt)
        c8 = k.sb("c8", [128, 2, ncc]); c16 = k.sb("c16", [128, 2, ncc]); Bc8 = Buf("c8")
        P.op("scalar", _m("activation", out=c8[:], in_=lamt[:], func=AF.Exp, scale=-1.0), reads=[Blam], writes=[Bc8])
        P.op("scalar", _m("activation", out=c8[:], in_=c8[:], func=AF.Ln, bias=1.0, scale=1.0), reads=[Bc8], writes=[Bc8])
        P.op("vector", _m("tensor_scalar", out=c16[:], in0=c8[:], scalar1=-16.0, scalar2=None, op0=ALU.mult), reads=[Bc8], writes=[Bc8])
        P.op("vector", _m("tensor_scalar", out=c8[:], in0=c8[:], scalar1=-8.0, scalar2=None, op0=ALU.mult), reads=[Bc8], writes=[Bc8])

        rt_r = Ring(k, "rt", 2, [128, LW + 3], F32)
        u_r = Ring(k, "u", 2, [128, LW], F32)
        ub_r = Ring(k, "ub", 3, [128, LW], BF16)
        rgt_r = Ring(k, "rgt", 2, [128, LW], F32)
        igt_r = Ring(k, "igt", 2, [128, LW], F32)
        a_r = Ring(k, "a", 2, [128, LW], F32)
        s_r = Ring(k, "s", 2, [128, LW], F32)
        h_r = Ring(k, "hh", 2, [128, LW], F32)
        g_r = Ring(k, "gg", 2, [128, LW], F32)
        h0_r = Ring(k, "h0", 2, [128, LW], F32)
        o_r = Ring(k, "ob", 2, [128, LW], BF16)
        st = k.sb("state", [128, 1], F32); Bst = Buf("state")
        pa = Ring(k, "pa", 3, [128, 512], F32, psum=True)
        px = Ring(k, "px", 3, [128, 512], F32, psum=True)
        ctx_t, lat_t = lru_tiles()
        def stage_a(cc, d, c0, W, s0, s1):
            rows = slice(cc * 128, (cc + 1) * 128)
            rrows = slice(1024 + cc * 128, 1024 + (cc + 1) * 128)
            rt, Brt = rt_r.next()
            lo = 0 if not s0 else 2
            hi = W + 3 if not s1 else W + 2
            if s0:
                P.op("gpsimd", _m("memset", rt[:, 0:2], 0.0), writes=[Brt])
            if s1:
                P.op("gpsimd", _m("memset", rt[:, W + 2:W + 3], 0.0), writes=[Brt])
            P.dma("sync", rt[:, lo:hi], proj[rrows, c0 - 2 + lo:c0 - 2 + hi], reads=[Bproj], writes=[Brt], sem=Brt)
            u, Bu = u_r.next()
            P.op("vector", _m("tensor_scalar",
                out=u[:, :W], in0=rt[:, 0:W], scalar1=cwt[:, cc, 0:1], scalar2=cwt[:, cc, 4:5], op0=ALU.mult, op1=ALU.add),
                reads=[Brt, Bcw], writes=[Bu])
            for kk in range(1, 4):
                P.op("vector", _m("scalar_tensor_tensor",
                    out=u[:, :W], in0=rt[:, kk:kk + W], scalar=cwt[:, cc, kk:kk + 1], in1=u[:, :W], op0=ALU.mult, op1=ALU.add),
                    reads=[Brt, Bcw, Bu], writes=[Bu])
            ub, Bub = ub_r.next()
            P.op("scalar", _m("activation", out=ub[:, :W], in_=u[:, :W], func=AF.Copy), reads=[Bu], writes=[Bub])
            rgt, Brg = rgt_r.next(); igt, Big = igt_r.next()
            for s in range(0, W, 512):
                n = min(512, W - s)
                psa, Bpa = pa.next(); psx, Bpx = px.next()
                P.op("tensor", _m("matmul",
                    psa[:, :n], lhsT=wt[:, 0, d, cc, :], rhs=ub[:, s:s + n], start=True, stop=True), reads=[Bwt, Bub], writes=[Bpa])
                P.op("tensor", _m("matmul",
                    psx[:, :n], lhsT=wt[:, 1, d, cc, :], rhs=ub[:, s:s + n], start=True, stop=True), reads=[Bwt, Bub], writes=[Bpx])
                P.op("scalar", _m("activation",
                    out=rgt[:, s:s + n], in_=psa[:, :n], func=AF.Sigmoid, bias=babt[:, 0, d, cc:cc + 1], scale=1.0),
                    reads=[Bpa, Bbab], writes=[Brg])
                P.op("scalar", _m("activation",
                    out=igt[:, s:s + n], in_=psx[:, :n], func=AF.Sigmoid, bias=babt[:, 1, d, cc:cc + 1], scale=1.0),
                    reads=[Bpx, Bbab], writes=[Big])
            a, Ba = a_r.next(); sq, Bs = s_r.next()
            P.op("scalar", _m("activation",
                out=a[:, :W], in_=rgt[:, :W], func=AF.Exp, scale=c8[:, d, cc:cc + 1]), reads=[Brg, Bc8], writes=[Ba])
            P.op("scalar", _m("activation",
                out=sq[:, :W], in_=rgt[:, :W], func=AF.Exp, scale=c16[:, d, cc:cc + 1]), reads=[Brg, Bc8], writes=[Bs])
            P.op("scalar", _m("activation", out=sq[:, :W], in_=sq[:, :W], func=AF.Sqrt, bias=1.0, scale=-1.0),
                 reads=[Bs], writes=[Bs])
            return dict(rows=rows, u=u, Bu=Bu, igt=igt, Big=Big, a=a, Ba=Ba, sq=sq, Bs=Bs)

        def stage_b(cc, d, c0, W, first, v):
            rows, u, Bu, igt, Big, a, Ba, sq, Bs = v['rows'], v['u'], v['Bu'], v['igt'], v['Big'], v['a'], v['Ba'], v['sq'], v['Bs']
            if first:
                P.op("gpsimd", _m("memset", st[:], 0.0), writes=[Bst])
            P.op("vector", _m("tensor_tensor", out=igt[:, :W], in0=igt[:, :W], in1=u[:, :W], op=ALU.mult),
                 reads=[Big, Bu], writes=[Big])
            P.op("gpsimd", _m("tensor_tensor", out=igt[:, :W], in0=igt[:, :W], in1=sq[:, :W], op=ALU.mult),
                 reads=[Big, Bs], writes=[Big])
            h, Bhh = h_r.next()
            if d == 0:
                P.op("vector", _m("tensor_tensor_scan",
                    out=h[:, :W], data0=a[:, :W], data1=igt[:, :W], initial=st[:, 0:1], op0=ALU.mult, op1=ALU.add),
                    reads=[Ba, Big, Bst], writes=[Bhh])
                P.op("vector", _m("tensor_copy", out=st[:, 0:1], in_=h[:, W - 1:W]), reads=[Bhh], writes=[Bst])
                P.dma("sync", h0s[rows, c0:c0 + W], h[:, :W], reads=[Bhh], writes=[Bh0s], sem=Bhh)
            else:
                P.op("vector", _m("tensor_tensor_scan",
                    out=h[:, W - 1::-1] if False else h[:, 0:W][:, ::-1], data0=a[:, 0:W][:, ::-1], data1=igt[:, 0:W][:, ::-1],
                    initial=st[:, 0:1], op0=ALU.mult, op1=ALU.add),
                    reads=[Ba, Big, Bst], writes=[Bhh])
                P.op("vector", _m("tensor_copy", out=st[:, 0:1], in_=h[:, 0:1]), reads=[Bhh], writes=[Bst])
                h0, Bh0 = h0_r.next(); g, Bg = g_r.next()
                P.dma("scalar", h0[:, :W], h0s[rows, c0:c0 + W], reads=[Bh0s], writes=[Bh0], sem=Bh0)
                P.dma("scalar", g[:, :W], proj[rows, c0:c0 + W], reads=[Bproj], writes=[Bg], sem=Bg)
                P.op("gpsimd", _m("tensor_tensor", out=h0[:, :W], in0=h0[:, :W], in1=h[:, :W], op=ALU.add),
                     reads=[Bhh, Bh0], writes=[Bh0])
                P.op("scalar", _m("activation", out=g[:, :W], in_=g[:, :W], func=AF.Gelu_apprx_tanh), reads=[Bg], writes=[Bg])
                ob, Bob = o_r.next()
                P.op("vector", _m("tensor_tensor", out=ob[:, :W], in0=g[:, :W], in1=h0[:, :W], op=ALU.mult),
                     reads=[Bg, Bh0], writes=[Bob])
                P.dma("sync", out[rows, c0:c0 + W], ob[:, :W], reads=[Bob], writes=[Bout], sem=Bob)

        jobs = []
        for cc in range(ncc):
            for d in range(2):
                order = ctx_t + lat_t if d == 0 else ctx_t + lat_t[::-1]
                for ti, (c0, W, s0, s1) in enumerate(order):
                    jobs.append((cc, d, c0, W, s0, s1, ti == 0))
        prev = None
        for (cc, d, c0, W, s0, s1, first) in jobs:
            cur = (cc, d, c0, W, first, stage_a(cc, d, c0, W, s0, s1))
            if prev is not None:
                stage_b(*prev)
            prev = cur
        stage_b(*prev)
RC = 128


def ret_tiles():
    ctx = [(0, LC, False, 0)]
    lat = [(LC + i * 512, 512, True, i * 512) for i in range(S_ // 512)]
    return ctx, lat


def host_ret_consts():
    j = np.arange(128, dtype=np.float32)[:, None]
    i = np.arange(128, dtype=np.float32)[None, :]
    dmat = np.stack([np.maximum(i - j, 0) + 0 * j, (i >= j).astype(np.float32), np.maximum(j - i, 0) + 0 * i, (j >= i).astype(np.float32)], 1)
    iq = np.stack([np.broadcast_to(i + 1.0, (128, 128)), np.broadcast_to(RC - i, (128, 128))], 1)
    jk = np.concatenate([RC - 1.0 - j, j], 1)
    return {"dmat": np.ascontiguousarray(dmat, np.float32), "iq": np.ascontiguousarray(iq, np.float32),
            "jk": np.ascontiguousarray(jk, np.float32), "ident": np.eye(128, dtype=np.float32)}


def host_rot1_tables():
    n_r = 128
    f_r = (10000.0 ** (-np.arange(n_r, dtype=np.float32) / n_r)).astype(np.float32)
    ang = np.arange(S_, dtype=np.float32)[:, None] * f_r
    return np.ascontiguousarray(np.cos(ang).T.astype(np.float32)), np.ascontiguousarray(np.sin(ang).T.astype(np.float32))


def emit_B2(k, t, hp):
    P = k.P
    proj, Bproj = t["proj"], t["Bproj"]
    cosT, sinT = t["cos1T"], t["sin1T"]
    dmat, iq, jk, ident = t["dmat"], t["iq"], t["jk"], t["ident"]
    dlb, gng = t["dlb"][:, :, 2 * hp:2 * hp + 2], t["gng"][:, 2 * hp:2 * hp + 2, :]
    out, Bout = t["mixA"], t["BmixA"]
    o0s = t["o0s"]; Bo0s = Buf("o0s")
    src = [proj[2048 + a * 1024 + hp * 512:2048 + a * 1024 + (hp + 1) * 512, :].rearrange("(a p) t -> p a t", p=128) for a in range(4)]
    o0v = o0s.rearrange("(a p) t -> p a t", p=128)
    outv = out[1024 + hp * 512:1024 + (hp + 1) * 512, :].rearrange("(a p) t -> p a t", p=128)
    if True:
        cs = load_consts(k, [("dmat", [128, 4, 128]), ("iq", [128, 2, 128]), ("jk", [128, 2]), ("ident", [128, 128]),
                             ("dlb", [128, 2, 2]), ("gng", [128, 2, 2])],
                         {"dmat": dmat, "iq": iq, "jk": jk, "ident": ident, "dlb": dlb, "gng": gng})
        dm, Bdm = cs["dmat"]; iqt, Biq = cs["iq"]; jkt, Bjk = cs["jk"]; idt, Bid = cs["ident"]; dl, Bdl = cs["dlb"]; gn, Bgn = cs["gng"]
        lg = k.sb("lg", [128, 2, 2]); Blg = Buf("lg")
        P.op("scalar", _m("activation", out=lg[:], in_=dl[:], func=AF.Exp, scale=-1.0), reads=[Bdl], writes=[Blg])
        P.op("scalar", _m("activation", out=lg[:], in_=lg[:], func=AF.Ln, bias=1.0, scale=1.0), reads=[Blg], writes=[Blg])
        P.op("vector", _m("tensor_scalar", out=lg[:], in0=lg[:], scalar1=-1.0, scalar2=None, op0=ALU.mult), reads=[Blg], writes=[Blg])
        mask = k.sb("mask", [128, 2, 2, 128], BF16); mtmp = k.sb("mtmp", [128, 128]); Bmask = Buf("mask"); Bmt = Buf("mtmp")
        qdec = k.sb("qdec", [128, 2, 2, 512], BF16); Bqdec = Buf("qdec")
        kdec = k.sb("kdec", [128, 2, 2]); sdec = k.sb("sdec", [128, 2, 2]); Bkd = Buf("kdec")
        c128 = k.sb("c128", [128, 1]); Bc128 = Buf("c128")
        P.op("gpsimd", _m("memset", c128[:], float(RC)), writes=[Bc128])
        for d in range(2):
            for hl in range(2):
                sc = lg[:, d, hl:hl + 1]
                P.op("scalar", _m("activation", out=mtmp[:], in_=dm[:, 2 * d, :], func=AF.Exp, scale=sc),
                     reads=[Bdm, Blg, Bmt], writes=[Bmt])
                P.op("vector", _m("tensor_tensor", out=mask[:, d, hl, :], in0=mtmp[:], in1=dm[:, 2 * d + 1, :], op=ALU.mult),
                     reads=[Bmt, Bdm], writes=[Bmask])
                P.op("scalar", _m("activation", out=mtmp[:], in_=iqt[:, d, :], func=AF.Exp, scale=sc),
                     reads=[Biq, Blg, Bmt], writes=[Bmt])
                for rep in range(4):
                    P.op("vector", _m("tensor_copy", out=qdec[:, d, hl, rep * 128:(rep + 1) * 128], in_=mtmp[:]),
                         reads=[Bmt], writes=[Bqdec])
                P.op("scalar", _m("activation", out=kdec[:, d, hl:hl + 1], in_=jkt[:, d:d + 1], func=AF.Exp, scale=sc),
                     reads=[Bjk, Blg], writes=[Bkd])
                P.op("scalar", _m("activation", out=sdec[:, d, hl:hl + 1], in_=c128[:], func=AF.Exp, scale=sc),
                     reads=[Bc128, Blg], writes=[Bkd])
        P.op("vector", _m("tensor_scalar", out=kdec[:], in0=kdec[:], scalar1=float(RET_KSCALE), scalar2=None, op0=ALU.mult),
             reads=[Bkd], writes=[Bkd])
        ones32 = k.sb("ones32", [128, 128]); Bo32 = Buf("ones32")
        P.op("gpsimd", _m("memset", ones32[:], 1.0), writes=[Bo32])
        epsb = k.sb("epsb", [128, 1]); Beps = Buf("eps")
        P.op("gpsimd", _m("memset", epsb[:], EPS), writes=[Beps])

        q_r = Ring(k, "q", 2, [128, 4, 512], F32); k_r = Ring(k, "k", 2, [128, 4, 512], F32); v_r = Ring(k, "v", 2, [128, 4, 512], F32)
        cs_r = Ring(k, "cs", 2, [128, 2, 512], F32)
        qb_r = Ring(k, "qb", 2, [128, 4, 512], BF16); kb_r = Ring(k, "kb", 2, [128, 4, 512], BF16)
        kr_r = Ring(k, "kr", 2, [128, 4, 512], F32)
        qd_r = Ring(k, "qd", 2, [128, 4, 512], BF16)
        vtm_r = Ring(k, "vtm", 2, [128, 4, 512], BF16); ktm_r = Ring(k, "ktm", 2, [128, 4, 512], BF16)
        t_r = Ring(k, "rt", 4, [128, 512], F32)
        tp_r = Ring(k, "rtp", 4, [128, 512], F32)
        msk_r = Ring(k, "msk", 6, [128, 128], BF16)
        S32r = [Ring(k, "S32_%d_" % h, 2, [128, 2, 256], F32) for h in range(2)]
        S32 = [None, None]; BS32 = [None, None]
        Sb_r = [Ring(k, "Sb%d_" % h, 4, [128, 2, 256], BF16) for h in range(2)]
        ost_r = Ring(k, "ost", 2, [128, 2, 512], F32)
        o0_r = Ring(k, "o0", 2, [128, 2, 512], F32)
        ol_r = Ring(k, "ol", 2, [128, 4, 512], F32)
        sq_r = Ring(k, "osq", 2, [128, 512], F32)
        mu = k.sb("mu", [128, 512]); Bmu = Buf("mu"); var = k.sb("var", [128, 512]); Bvar = Buf("var")
        y_r = Ring(k, "yb", 2, [128, 2, 512], BF16)
        psc = k.ps("sc"); Bsc = [Buf("sc%d" % i) for i in range(4)]
        po = [[k.ps("po%d%d" % (h, e_)) for e_ in range(2)] for h in range(2)]
        Bpo = [[Buf("po%d%d" % (h, e_)) for e_ in range(2)] for h in range(2)]
        pu = k.ps("pu"); Bpu = Buf("pu")
        ptv = k.ps("ptv"); Bptv = Buf("ptv"); ptk = k.ps("ptk"); Bptk = Buf("ptk")
        pst1, Bp1, pst2, Bp2 = ptv, Bptv, ptk, Bptk

        def eng_alt(i):
            return "vector" if i % 2 == 0 else "gpsimd"

        ctx_t, lat_t = ret_tiles()
        Sb_cur = [None, None]

        def stage_a(d, c0, W, rot, pos0):
            nch = W // RC
            q, Bq = q_r.next(); kk_, Bk = k_r.next(); v, Bv = v_r.next()
            P.dma("sync", q[:, :, :W], src[0][:, :, c0:c0 + W], reads=[Bproj], writes=[Bq], sem=Bq)
            P.dma("scalar", kk_[:, :, :W], src[1][:, :, c0:c0 + W], reads=[Bproj], writes=[Bk], sem=Bk)
            P.dma("sync", v[:, :, :W], src[2][:, :, c0:c0 + W], reads=[Bproj], writes=[Bv], sem=Bv)
            qb, Bqb = qb_r.next(); kb, Bkb = kb_r.next(); kr, Bkr = kr_r.next()
            if rot:
                cst, Bcs = cs_r.next()
                P.dma("scalar", cst[:, 0, :W], cosT[:, pos0:pos0 + W], writes=[Bcs], sem=Bcs)
                P.dma("scalar", cst[:, 1, :W], sinT[:, pos0:pos0 + W], writes=[Bcs], sem=Bcs)
                n = 0
                for (srct, Bsrc, dst, Bdst) in ((q, Bq, qb, Bqb), (kk_, Bk, kr, Bkr)):
                    for hl in range(2):
                        x1 = srct[:, 2 * hl, :W]; x2 = srct[:, 2 * hl + 1, :W]
                        ge = "gpsimd" if (srct is kk_ and hl == 1) else "vector"
                        tr_ = tp_r if ge == "gpsimd" else t_r
                        t1, Bt1 = tr_.next(); t2, Bt2 = tr_.next()
                        P.op(ge, _m("tensor_tensor", out=t1[:, :W], in0=x1, in1=cst[:, 0, :W], op=ALU.mult), reads=[Bsrc, Bcs], writes=[Bt1]); n += 1
                        P.op(ge, _m("tensor_tensor", out=t2[:, :W], in0=x2, in1=cst[:, 1, :W], op=ALU.mult), reads=[Bsrc, Bcs], writes=[Bt2]); n += 1
                        P.op(ge, _m("tensor_tensor", out=dst[:, 2 * hl, :W], in0=t1[:, :W], in1=t2[:, :W], op=ALU.subtract), reads=[Bt1, Bt2], writes=[Bdst]); n += 1
                        t3, Bt3 = tr_.next(); t4, Bt4 = tr_.next()
                        P.op(ge, _m("tensor_tensor", out=t3[:, :W], in0=x1, in1=cst[:, 1, :W], op=ALU.mult), reads=[Bsrc, Bcs], writes=[Bt3]); n += 1
                        P.op(ge, _m("tensor_tensor", out=t4[:, :W], in0=x2, in1=cst[:, 0, :W], op=ALU.mult), reads=[Bsrc, Bcs], writes=[Bt4]); n += 1
                        P.op(ge, _m("tensor_tensor", out=dst[:, 2 * hl + 1, :W], in0=t3[:, :W], in1=t4[:, :W], op=ALU.add), reads=[Bt3, Bt4], writes=[Bdst]); n += 1
                krs, Bkrs = kr, Bkr
            else:
                P.op("scalar", _m("activation", out=qb[:, :, :W], in_=q[:, :, :W], func=AF.Copy), reads=[Bq], writes=[Bqb])
                krs, Bkrs = kk_, Bk
            P.op("scalar", _m("activation", out=kb[:, :, :W], in_=krs[:, :, :W], func=AF.Copy, scale=float(RET_KSCALE)),
                 reads=[Bkrs], writes=[Bkb])
            qd, Bqd = qd_r.next()
            for hl in range(2):
                for dch in range(2):
                    P.op("vector", _m("tensor_tensor",
                        out=qd[:, 2 * hl + dch, :W], in0=qb[:, 2 * hl + dch, :W], in1=qdec[:, d, hl, :W], op=ALU.mult),
                        reads=[Bqb, Bqdec], writes=[Bqd])
            vtm, Bvtm = vtm_r.next(); ktm, Bktm = ktm_r.next()
            for c in range(nch):
                for a in range(4):
                    P.op("tensor", _m("transpose", out=ptv[:, a * 128:(a + 1) * 128], in_=v[:, a, c * 128:(c + 1) * 128], identity=idt[:]),
                         reads=[Bv, Bid], writes=[Bptv])
                P.op("scalar", _m("activation", out=vtm[:, c, :], in_=ptv[:], func=AF.Copy), reads=[Bptv], writes=[Bvtm])
                for a in range(4):
                    P.op("tensor", _m("transpose", out=ptk[:, a * 128:(a + 1) * 128], in_=krs[:, a, c * 128:(c + 1) * 128], identity=idt[:]),
                         reads=[Bkrs, Bid], writes=[Bptk])
                for hl in range(2):
                    P.op("vector", _m("tensor_scalar",
                        out=ktm[:, c, hl * 256:(hl + 1) * 256], in0=ptk[:, hl * 256:(hl + 1) * 256], scalar1=kdec[:, d, hl:hl + 1], scalar2=None, op0=ALU.mult),
                        reads=[Bptk, Bkd], writes=[Bktm])
            return dict(nch=nch, qb=qb, Bqb=Bqb, kb=kb, Bkb=Bkb, qd=qd, Bqd=Bqd, vtm=vtm, Bvtm=Bvtm, ktm=ktm, Bktm=Bktm)

        def stage_b(d, c0, W, first, v):
            nch, qb, Bqb, kb, Bkb, qd, Bqd = v['nch'], v['qb'], v['Bqb'], v['kb'], v['Bkb'], v['qd'], v['Bqd']
            vtm, Bvtm, ktm, Bktm = v['vtm'], v['Bvtm'], v['ktm'], v['Bktm']
            if first:
                for hl in range(2):
                    S32[hl], BS32[hl] = S32r[hl].next()
                    P.op("gpsimd", _m("memset", S32[hl][:], 0.0), writes=[BS32[hl]])
                    sb_, Bsb_ = Sb_r[hl].next()
                    P.op("gpsimd", _m("memset", sb_[:], 0.0), writes=[Bsb_])
                    Sb_cur[hl] = (sb_, Bsb_)
            chunks = list(range(nch)) if d == 0 else list(range(nch))[::-1]

            def front(c):
                res = []
                for hl in range(2):
                    cs_ = slice(c * 128, (c + 1) * 128)
                    sl_ = (2 * c + hl) % 4
                    ss_ = slice(sl_ * 128, (sl_ + 1) * 128)
                    for dch in range(2):
                        P.op("tensor", _m("matmul",
                            psc[:, ss_], lhsT=kb[:, 2 * hl + dch, cs_], rhs=qb[:, 2 * hl + dch, cs_], start=(dch == 0), stop=(dch == 1)),
                            reads=[Bkb, Bqb], writes=[Bsc[sl_]], sig=(dch == 1))
                    for dch in range(2):
                        P.op("tensor", _m("matmul",
                            pu[:, dch * 256:(dch + 1) * 256], lhsT=ktm[:, c, hl * 256 + dch * 128:hl * 256 + (dch + 1) * 128],
                            rhs=vtm[:, c, hl * 256:(hl + 1) * 256], start=True, stop=True),
                            reads=[Bktm, Bvtm], writes=[Bpu], sig=(dch == 1))
                    mk, Bmk = msk_r.next()
                    P.op("vector", _m("tensor_tensor", out=mk[:], in0=psc[:, ss_], in1=mask[:, d, hl, :], op=ALU.mult),
                         reads=[Bsc[sl_], Bmask], writes=[Bmk])
                    sbt, Bsbt = Sb_cur[hl]
                    So, BSo = S32[hl], BS32[hl]
                    Sn, BSn = S32r[hl].next()
                    P.op("vector", _m("scalar_tensor_tensor",
                        out=Sn[:].rearrange("p a b -> p (a b)"), in0=So[:].rearrange("p a b -> p (a b)"), scalar=sdec[:, d, hl:hl + 1],
                        in1=pu[:], op0=ALU.mult, op1=ALU.add), reads=[Bpu, Bkd, BSo], writes=[BSn])
                    S32[hl], BS32[hl] = Sn, BSn
                    nsb, Bnsb = Sb_r[hl].next()
                    P.op("scalar", _m("activation", out=nsb[:], in_=S32[hl][:], func=AF.Copy), reads=[BS32[hl]], writes=[Bnsb])
                    Sb_cur[hl] = (nsb, Bnsb)
                    res.append((mk, Bmk, sbt, Bsbt))
                return res

            def back(c, res):
                cs_ = slice(c * 128, (c + 1) * 128)
                for hl in range(2):
                    mk, Bmk, sbt, Bsbt = res[hl]
                    for ech in range(2):
                        P.op("tensor", _m("matmul",
                            po[hl][ech][:, cs_], lhsT=vtm[:, c, (2 * hl + ech) * 128:(2 * hl + ech + 1) * 128], rhs=mk[:], start=True, stop=False),
                            reads=[Bvtm, Bmk], writes=[Bpo[hl][ech]])
                        for dch in range(2):
                            P.op("tensor", _m("matmul",
                                po[hl][ech][:, cs_], lhsT=sbt[:, dch, ech * 128:(ech + 1) * 128], rhs=qd[:, 2 * hl + dch, cs_], start=False, stop=(dch == 1)),
                                reads=[Bsbt, Bqd], writes=[Bpo[hl][ech]])

            pending = front(chunks[0])
            for ci, c in enumerate(chunks):
                nxt_ = front(chunks[ci + 1]) if ci + 1 < len(chunks) else None
                back(c, pending)
                pending = nxt_
            for hl in range(2):
                if d == 0:
                    ost, Bost = ost_r.next()
                    for ech in range(2):
                        P.op("scalar" if ech == 0 else "vector",
                             (_m("activation", out=ost[:, ech, :W], in_=po[hl][ech][:, :W], func=AF.Copy)) if ech == 0 else
                             (_m("tensor_copy", out=ost[:, ech, :W], in_=po[hl][ech][:, :W])),
                             reads=[Bpo[hl][ech]], writes=[Bost])
                    P.dma("sync", o0v[:, 2 * hl:2 * hl + 2, c0:c0 + W], ost[:, :, :W], reads=[Bost], writes=[Bo0s], sem=Bost)
                else:
                    o0, Bo0 = o0_r.next()
                    P.dma("sync", o0[:, :, :W], o0v[:, 2 * hl:2 * hl + 2, c0:c0 + W], reads=[Bo0s], writes=[Bo0], sem=Bo0)
                    if hl == 0:
                        olt, Bol = ol_r.next()
                        P.dma("scalar", olt[:, :, :W], src[3][:, :, c0:c0 + W], reads=[Bproj], writes=[Bol], sem=Bol)
                    for ech in range(2):
                        P.op("vector", _m("tensor_tensor", out=o0[:, ech, :W], in0=o0[:, ech, :W], in1=po[hl][ech][:, :W], op=ALU.add),
                             reads=[Bo0, Bpo[hl][ech]], writes=[Bo0])
                    for ech in range(2):
                        P.op("tensor", _m("matmul", pst1[:, :W], lhsT=ones32[:], rhs=o0[:, ech, :W], start=(ech == 0), stop=(ech == 1)),
                             reads=[Bo0, Bo32], writes=[Bp1])
                    for ech in range(2):
                        sq, Bsq = sq_r.next()
                        P.op("scalar", _m("activation", out=sq[:, :W], in_=o0[:, ech, :W], func=AF.Square), reads=[Bo0], writes=[Bsq])
                        P.op("tensor", _m("matmul", pst2[:, :W], lhsT=ones32[:], rhs=sq[:, :W], start=(ech == 0), stop=(ech == 1)),
                             reads=[Bsq, Bo32], writes=[Bp2])
                    P.op("vector", _m("tensor_scalar", out=mu[:, :W], in0=pst1[:, :W], scalar1=1.0 / 256, scalar2=None, op0=ALU.mult), reads=[Bp1], writes=[Bmu])
                    P.op("vector", _m("tensor_tensor", out=var[:, :W], in0=mu[:, :W], in1=mu[:, :W], op=ALU.mult), reads=[Bmu], writes=[Bvar])
                    P.op("vector", _m("scalar_tensor_tensor", out=var[:, :W], in0=pst2[:, :W], scalar=1.0 / 256, in1=var[:, :W], op0=ALU.mult, op1=ALU.subtract),
                         reads=[Bp2, Bvar], writes=[Bvar])
                    P.op("scalar", _m("activation", out=var[:, :W], in_=var[:, :W], func=AF.Sqrt, bias=epsb[:, 0:1], scale=1.0), reads=[Bvar, Beps], writes=[Bvar])
                    P.op("vector", _m("reciprocal", out=var[:, :W], in_=var[:, :W]), reads=[Bvar], writes=[Bvar])
                    yb, Byb = y_r.next()
                    for ech in range(2):
                        P.op("vector", _m("tensor_tensor", out=o0[:, ech, :W], in0=o0[:, ech, :W], in1=mu[:, :W], op=ALU.subtract), reads=[Bo0, Bmu], writes=[Bo0])
                        P.op("gpsimd", _m("tensor_tensor", out=o0[:, ech, :W], in0=o0[:, ech, :W], in1=var[:, :W], op=ALU.mult), reads=[Bo0, Bvar], writes=[Bo0])
                        P.op("scalar", _m("activation", out=olt[:, 2 * hl + ech, :W], in_=olt[:, 2 * hl + ech, :W], func=AF.Silu), reads=[Bol], writes=[Bol])
                        P.op("vector", _m("scalar_tensor_tensor",
                            out=yb[:, ech, :W], in0=o0[:, ech, :W], scalar=gn[:, hl, ech:ech + 1], in1=olt[:, 2 * hl + ech, :W], op0=ALU.mult, op1=ALU.mult),
                            reads=[Bo0, Bgn, Bol], writes=[Byb])
                    P.dma("sync", outv[:, 2 * hl:2 * hl + 2, c0:c0 + W], yb[:, :, :W], reads=[Byb], writes=[Bout], sem=Byb)

        jobs = []
        for d in range(2):
            order = ctx_t + (lat_t if d == 0 else lat_t[::-1])
            for ti, (c0, W, rot, pos0) in enumerate(order):
                jobs.append((d, c0, W, rot, pos0, ti == 0))
        prev = None
        for (d, c0, W, rot, pos0, first) in jobs:
            cur = (d, c0, W, first, stage_a(d, c0, W, rot, pos0))
            if prev is not None:
                stage_b(*prev)
            prev = cur
        stage_b(*prev)

RET_KSCALE = 256 ** -0.5

POOL_WINDOWS = (2, 4, 8, 16)
NKC = NTOT // 128
ATT_SCALE = 128 ** -0.5


def host_rot2_tables():
    rows = S_ // 64
    row = np.repeat(np.arange(rows, dtype=np.float32), 64)
    col = np.tile(np.arange(64, dtype=np.float32), rows)
    n_ax = 32
    f_ax = (10000.0 ** (-np.arange(n_ax, dtype=np.float32) / n_ax)).astype(np.float32)
    ang = np.concatenate([row[:, None] * f_ax, col[:, None] * f_ax], -1)
    c = np.cos(ang).astype(np.float32).T; s = np.sin(ang).astype(np.float32).T
    cosF = np.concatenate([c, c], 0); sinS = np.concatenate([-s, s], 0)
    pm = np.zeros((128, 128), np.float32)
    for dp in range(128):
        pm[(dp + 64) % 128, dp] = 1.0
    return np.ascontiguousarray(cosF), np.ascontiguousarray(sinS), pm


def host_pool_icnt(half):
    t = np.arange(S_)
    out = np.zeros((128, 4, NLAT), np.float32)
    for g, w in enumerate(POOL_WINDOWS):
        lo = np.clip(t - w // 2, 0, S_); hi = np.clip(t + w // 2, 0, S_)
        out[:, g, :] = (1.0 / (hi - lo).astype(np.float32))[None, half * NLAT:(half + 1) * NLAT]
    return out


def host_pool_slice(pool_rows, half):
    pad = np.zeros((512, S_ + 16), np.float32)
    pad[:, 8:8 + S_] = pool_rows
    return np.ascontiguousarray(pad[:, half * NLAT:half * NLAT + NLAT + 16].reshape(4, 128, NLAT + 16))


def emit_D(k, t):
    P = k.P
    NKV, NQH, NQT = 4, 12, NLAT // 512
    poolS, BpoolS = t["poolS"], t["BpoolS"]
    kvS, BkvS = t["kvS"], t["BkvS"]
    qS, BqS = t["qS"], t["BqS"]
    cosF, sinS, pm, ident = t["cosF"], t["sinS"], t["pm"], t["ident"]
    cosQ, sinQ = t["cosQ"], t["sinQ"]
    icnt, hmask = t["icnt"], t["hmask"]
    pw, psc_, qkg = t["pw"], t["pscale"], t["qkg"]
    mixB, BmixB = t["mixB"], t["BmixB"]
    if True:
        cs = load_consts(k, [("pm", [128, 128]), ("ident", [128, 128]), ("pscale", [128, 4]), ("qkg", [128, 2]), ("hmask", [128, 16])],
                         {"pm": pm, "ident": ident, "pscale": psc_, "qkg": qkg, "hmask": hmask})
        hmt, Bhm = cs["hmask"]
        pmt, Bpm = cs["pm"]; idt, Bid = cs["ident"]; pst_, Bpsc = cs["pscale"]; qkgt, Bqkg = cs["qkg"]
        pwt = k.sb("pwt", [128, 4, 128], BF16); Bpw = Buf("pw")
        P.dma("gpsimd", pwt[:], pw, writes=[Bpw], sem=Bpw)
        ones32 = k.sb("ones32", [128, 128]); Bo32 = Buf("o32")
        P.op("gpsimd", _m("memset", ones32[:], 1.0), writes=[Bo32])
        onesb = k.sb("onesb", [128, 128], BF16); Bob = Buf("ob")
        P.op("gpsimd", _m("memset", onesb[:], 1.0), writes=[Bob])
        epsb = k.sb("epsb", [128, 1]); Beps = Buf("eps")
        P.op("gpsimd", _m("memset", epsb[:], EPS), writes=[Beps])
        qgs = k.sb("qgs", [128, 1]); Bqgs = Buf("qgs")
        P.op("vector", _m("tensor_scalar", out=qgs[:], in0=qkgt[:, 0:1], scalar1=float(ATT_SCALE), scalar2=None, op0=ALU.mult), reads=[Bqkg], writes=[Bqgs])
        pmisc = Ring(k, "pmisc", 2, [128, 512], F32, psum=True)
        pscr = Ring(k, "pscr", 4, [128, 512], F32, psum=True)
        po = k.ps("po"); Bpo = Buf("po"); pden = k.ps("pden"); Bpden = Buf("pden")
        x_r = Ring(k, "xin", 2, [128, 512], F32)
        cs_r = Ring(k, "cst", 2, [128, 2, 512], F32)
        sq_r = Ring(k, "sq", 2, [128, 512], F32)
        rs_r = Ring(k, "rs", 2, [128, 512], F32)
        xn_r = Ring(k, "xn", 2, [128, 512], F32)
        t_r = Ring(k, "tt", 3, [128, 512], F32)
        qb_r = Ring(k, "qb", 2, [128, 512], BF16)
        pT_r = Ring(k, "pT", 6, [128, 512], BF16)
        ob_r = Ring(k, "ob", 2, [128, 512], BF16)
        rd_r = Ring(k, "rd", 2, [128, 512], F32)
        kb = [k.sb("kb%d" % i, [128, NTOT], BF16) for i in range(NKV)]; Bkb = [Buf("kb%d" % i) for i in range(NKV)]
        vtm = [k.sb("vtm%d" % i, [128, NKC, 128], BF16) for i in range(NKV)]; Bvtm = [Buf("vtm%d" % i) for i in range(NKV)]

        def normrot(src_ap, Bsrc, W, gcol, Bg, pos0, dst_ap, Bdst, cst, Bcs):
            sq, Bsq = sq_r.next()
            P.op("scalar", _m("activation", out=sq[:, :W], in_=src_ap, func=AF.Square), reads=[Bsrc], writes=[Bsq])
            ps, Bps = pmisc.next()
            P.op("tensor", _m("matmul", ps[:, :W], lhsT=ones32[:], rhs=sq[:, :W], start=True, stop=True), reads=[Bsq, Bo32], writes=[Bps])
            rs, Brs = rs_r.next()
            P.op("scalar", _m("activation", out=rs[:, :W], in_=ps[:, :W], func=AF.Sqrt, bias=epsb[:, 0:1], scale=1.0 / 128), reads=[Bps, Beps], writes=[Brs])
            P.op("vector", _m("reciprocal", out=rs[:, :W], in_=rs[:, :W]), reads=[Brs], writes=[Brs])
            if pos0 is None:
                P.op("vector", _m("scalar_tensor_tensor", out=dst_ap, in0=src_ap, scalar=gcol, in1=rs[:, :W], op0=ALU.mult, op1=ALU.mult),
                     reads=[Bsrc, Bg, Brs], writes=[Bdst])
                return
            xn, Bxn = xn_r.next()
            P.op("vector", _m("scalar_tensor_tensor", out=xn[:, :W], in0=src_ap, scalar=gcol, in1=rs[:, :W], op0=ALU.mult, op1=ALU.mult),
                 reads=[Bsrc, Bg, Brs], writes=[Bxn])
            pp, Bpp = pmisc.next()
            P.op("tensor", _m("matmul", pp[:, :W], lhsT=pmt[:], rhs=xn[:, :W], start=True, stop=True), reads=[Bxn, Bpm], writes=[Bpp])
            t1, Bt1 = t_r.next(); t2, Bt2 = t_r.next()
            P.op("gpsimd", _m("tensor_tensor", out=t1[:, :W], in0=xn[:, :W], in1=cst[:, 0, :W], op=ALU.mult), reads=[Bxn, Bcs], writes=[Bt1])
            P.op("vector", _m("tensor_tensor", out=t2[:, :W], in0=pp[:, :W], in1=cst[:, 1, :W], op=ALU.mult), reads=[Bpp, Bcs], writes=[Bt2])
            P.op("vector", _m("tensor_tensor", out=dst_ap, in0=t1[:, :W], in1=t2[:, :W], op=ALU.add), reads=[Bt1, Bt2], writes=[Bdst])

        def load_cs(pos0, W, own=False):
            cst, Bcs = cs_r.next()
            P.dma("scalar", cst[:, 0, :W], (cosQ if own else cosF)[:, pos0:pos0 + W], writes=[Bcs], sem=Bcs)
            P.dma("scalar", cst[:, 1, :W], (sinQ if own else sinS)[:, pos0:pos0 + W], writes=[Bcs], sem=Bcs)
            return cst, Bcs

        PW = 512
        xfull = k.sb("xfull", [128, NLAT + 16], F32); Bxf = Buf("xfull")
        s_r = [Ring(k, "ps%d_" % i, 1, [128, PW + 16], F32) for i in range(2)]
        ic_r = Ring(k, "ic", 2, [128, PW], F32)
        df_r = Ring(k, "df", 2, [128, PW], BF16)
        for gl in range(4):
            w = POOL_WINDOWS[gl]
            P.dma("sync", xfull[:], (lambda h1, gl=gl: poolS[gl][bass.ds(h1, 1), :, :].rearrange("o p t -> p (o t)")),
                  reads=[BpoolS], writes=[Bxf], sem=Bxf)
            for t0 in range(0, NLAT, PW):
                xp, Bxp = xfull[:, t0:t0 + PW + 16], Bxf
                ic, Bic = ic_r.next()
                P.dma("scalar", ic[:], icnt[:, gl, t0:t0 + PW], writes=[Bic], sem=Bic)
                cur, Bcur, span, n = xp, Bxp, 1, PW + 16
                i = 0
                while span < w:
                    nxt, Bnxt = s_r[i % 2].next(); i += 1
                    n2 = n - span
                    P.op("vector", _m("tensor_tensor", out=nxt[:, :n2], in0=cur[:, 0:n2], in1=cur[:, span:span + n2], op=ALU.add), reads=[Bcur], writes=[Bnxt])
                    cur, Bcur, span, n = nxt, Bnxt, span * 2, n2
                off = 8 - w // 2
                P.op("gpsimd", _m("tensor_tensor", out=ic[:], in0=cur[:, off:off + PW], in1=ic[:], op=ALU.mult), reads=[Bcur, Bic], writes=[Bic])
                df, Bdf = df_r.next()
                P.op("vector", _m("tensor_tensor", out=df[:], in0=ic[:], in1=xp[:, 8:8 + PW], op=ALU.subtract), reads=[Bic, Bxp], writes=[Bdf])
                for s in range(0, PW, 512):
                    ps, Bps = pmisc.next()
                    P.op("tensor", _m("matmul", ps[:], lhsT=pwt[:, gl, :], rhs=df[:, s:s + 512], start=True, stop=True), reads=[Bpw, Bdf], writes=[Bps])
                    ob, Bob_ = ob_r.next()
                    P.op("scalar", _m("activation", out=ob[:], in_=ps[:], func=AF.Copy, scale=1.0) if False else
                         _m("activation", out=ob[:], in_=ps[:], func=AF.Identity, scale=pst_[:, gl:gl + 1], bias=0.0), reads=[Bps, Bpsc], writes=[Bob_])
                    P.dma("sync", mixB[gl * 128:(gl + 1) * 128, t0 + s:t0 + s + 512], ob[:], reads=[Bob_], writes=[BmixB], sem=Bob_)

        ktiles = [(0, LC, None)] + [(LC + i * 512, 512, i * 512) for i in range(S_ // 512)]
        for kv in range(NKV):
            for (c0, W, pos0) in ktiles:
                x, Bx_ = x_r.next()
                P.dma("sync", x[:, :W], kvS[kv * 128:(kv + 1) * 128, c0:c0 + W], reads=[BkvS], writes=[Bx_], sem=Bx_)
                cst, Bcs = load_cs(pos0, W) if pos0 is not None else (None, None)
                normrot(x[:, :W], Bx_, W, qkgt[:, 1:2], Bqkg, pos0, kb[kv][:, c0:c0 + W], Bkb[kv], cst, Bcs)
                v, Bv = x_r.next()
                P.dma("sync", v[:, :W], kvS[512 + kv * 128:512 + (kv + 1) * 128, c0:c0 + W], reads=[BkvS], writes=[Bv], sem=Bv)
                pt, Bpt = pmisc.next()
                for c in range(W // 128):
                    P.op("tensor", _m("transpose", out=pt[:, c * 128:(c + 1) * 128], in_=v[:, c * 128:(c + 1) * 128], identity=idt[:]), reads=[Bv, Bid], writes=[Bpt])
                P.op("scalar", _m("activation", out=vtm[kv][:, c0 // 128:(c0 + W) // 128, :].rearrange("p a b -> p (a b)"), in_=pt[:, :W], func=AF.Copy),
                     reads=[Bpt], writes=[Bvtm[kv]])

        cs_cache = {}

        def prep_q(qt, h):
            pos0 = qt * 512
            if qt not in cs_cache:
                cs_cache.clear()
                cs_cache[qt] = load_cs(pos0, 512, own=True)
            cst, Bcs = cs_cache[qt]
            x, Bx_ = x_r.next()
            P.dma("sync", x[:], qS[h * 128:(h + 1) * 128, pos0:pos0 + 512], reads=[BqS], writes=[Bx_], sem=Bx_)
            qb, Bqb = qb_r.next()
            normrot(x[:], Bx_, 512, qgs[:, 0:1], Bqgs, pos0, qb[:], Bqb, cst, Bcs)
            return qb, Bqb

        jobs = [(qt, h) for qt in range(NQT) for h in range(NQH)]
        nxt = prep_q(*jobs[0])
        for ji, (qt, h) in enumerate(jobs):
            if True:
                pos0 = qt * 512
                kv = h // 3
                qb, Bqb = nxt
                if ji + 1 < len(jobs):
                    nxt = prep_q(*jobs[ji + 1])
                LAG = 3
                pend = []
                for sc in range(NKC + LAG):
                    if sc < NKC:
                        ps, Bps = pscr.next()
                        P.op("tensor", _m("matmul", ps[:], lhsT=kb[kv][:, sc * 128:(sc + 1) * 128], rhs=qb[:], start=True, stop=True), reads=[Bkb[kv], Bqb], writes=[Bps])
                        pT, BpT = pT_r.next()
                        P.op("scalar", _m("activation", out=pT[:], in_=ps[:], func=AF.Exp), reads=[Bps], writes=[BpT])
                        pend.append((sc, pT, BpT))
                    if sc >= LAG:
                        s2, pT2, BpT2 = pend.pop(0)
                        P.op("tensor", _m("matmul", po[:], lhsT=vtm[kv][:, s2, :], rhs=pT2[:], start=(s2 == 0), stop=(s2 == NKC - 1)), reads=[Bvtm[kv], BpT2], writes=[Bpo])
                        P.op("tensor", _m("matmul", pden[:], lhsT=onesb[:], rhs=pT2[:], start=(s2 == 0), stop=(s2 == NKC - 1)), reads=[Bob, BpT2], writes=[Bpden])
                rd, Brd = rd_r.next()
                P.op("vector", _m("reciprocal", out=rd[:], in_=pden[:]), reads=[Bpden], writes=[Brd])
                of, Bof = t_r.next()
                P.op("scalar", _m("activation", out=of[:], in_=po[:], func=AF.Copy), reads=[Bpo], writes=[Bof])
                ob, Bob_ = ob_r.next()
                P.op("gpsimd", _m("tensor_tensor", out=ob[:], in0=of[:], in1=rd[:], op=ALU.mult), reads=[Bof, Brd], writes=[Bob_])
                P.dma("sync", mixB[512 + h * 128:512 + (h + 1) * 128, pos0:pos0 + 512], ob[:], reads=[Bob_], writes=[BmixB], sem=Bob_)


def prep_mod_into(R, modv, Bmodv, npre, Bnpre, npost, Bnpost, nkinds, subs, wsteps):
    P = R.k.P
    R.Bmod = Buf("modABC1")
    for kd in range(nkinds):
        for sub in subs:
            sh = modv[:, sub * 48 + 0:sub * 48 + 16, kd]
            sc = modv[:, sub * 48 + 16:sub * 48 + 32, kd]
            gt = modv[:, sub * 48 + 32:sub * 48 + 48, kd]
            P.op("vector", _m("scalar_tensor_tensor", out=R.A[:, kd, sub, :], in0=sc, scalar=1.0, in1=npre[:, sub, :], op0=ALU.add, op1=ALU.mult),
                 reads=[Bmodv, Bnpre], writes=[R.Bmod])
            P.op("vector", _m("tensor_copy", out=R.Bv[:, kd, sub, :], in_=sh), reads=[Bmodv], writes=[R.Bmod])
            P.op("vector", _m("scalar_tensor_tensor", out=R.C[:, kd, sub, :], in0=gt, scalar=float(wsteps[sub]), in1=npost[:, sub, :], op0=ALU.mult, op1=ALU.mult),
                 reads=[Bmodv, Bnpost], writes=[R.Bmod])


@contextlib.contextmanager
def phase(nc, tag):
    with nc.cleanup_on_exit():
        k = K(nc, tag)
        with k.es:
            yield k
            k.P.barrier()
            k.P.build()


def emit_mod(k, t):
    P = k.P
    cT, mw, mb, modS = t["cT"], t["mw"], t["mb"], t["modS"]
    NCH = 288
    ct = k.sb("ct", [128, 16, 8]); Bct = Buf("ct")
    sct = k.sb("sct", [128, 16, 8], BF16); Bsct = Buf("sct")
    mbt = k.sb("mbt", [128, NCH]); Bmb = Buf("mb")
    res = k.sb("res", [128, NCH, 8]); Bres = Buf("res")
    wr = Ring(k, "mw", 6, [128, 16, 128], BF16)
    pr = Ring(k, "pm", 4, [128, 512], F32, psum=True)
    P.dma("sync", ct[:], cT, writes=[Bct], sem=Bct)
    P.dma("sync", mbt[:], mb, writes=[Bmb], sem=Bmb)
    P.op("scalar", _m("activation", out=sct[:], in_=ct[:], func=AF.Silu), reads=[Bct], writes=[Bsct])
    for j in range(NCH):
        wt, Bw = wr.next()
        P.dma("gpsimd", wt[:], mw[j], writes=[Bw], sem=Bw, max_dma_last_dim=8192)
        ps, Bps = pr.next()
        for kc in range(16):
            P.op("tensor", _m("matmul", ps[:, 0:8], lhsT=wt[:, kc, :], rhs=sct[:, kc, :], start=(kc == 0), stop=(kc == 15)),
                 reads=[Bw, Bsct], writes=[Bps])
        P.op("scalar", _m("activation", out=res[:, j, :], in_=ps[:, 0:8], func=AF.Identity, bias=mbt[:, j:j + 1], scale=1.0),
             reads=[Bps, Bmb], writes=[Bres])
    P.dma("sync", modS, res[:], reads=[Bres], sem=Bres)


def rl_setup(k, t, layers):
    R = RowLocal(k, TT); R.alloc_ffn(); R.init_eps()
    sets = {}
    for l in layers:
        cs = load_consts(k, [("modv%d" % l, [128, 144, 2]), ("npre%d" % l, [128, 3, 16]), ("npost%d" % l, [128, 3, 16])],
                         {"modv%d" % l: t["modS"][:, l * 144:(l + 1) * 144, 0:2], "npre%d" % l: t["npre"][l], "npost%d" % l: t["npost"][l]})
        R.A = k.sb("modA%d" % l, [128, 2, 3, 16], F32); R.Bv = k.sb("modB%d" % l, [128, 2, 3, 16], F32); R.C = k.sb("modC%d" % l, [128, 2, 3, 16], F32)
        prep_mod_into(R, cs["modv%d" % l][0], cs["modv%d" % l][1], cs["npre%d" % l][0], cs["npre%d" % l][1],
                      cs["npost%d" % l][0], cs["npost%d" % l][1], 2, [0, 1, 2], [0.5, 1.0, 0.5])
        sets[l] = (R.A, R.Bv, R.C, R.Bmod)

    def use(l):
        R.A, R.Bv, R.C, R.Bmod = sets[l]
    return R, use


def all_tiles():
    return [(0, LC, 1)] + [(LC + i * TT, TT, 0) for i in range(S_ // TT)]


def emit_A(k, t):
    R, use = rl_setup(k, t, [0]); use(0)
    for (c0, T, kd) in all_tiles():
        R.load_x(t["xT"], c0, T)
        R.ffn_sublayer(kd, 0, T, t["wgu"][0][0], t["wdn"][0][0])
        R.store_x(t["x1S"], c0, T)
        R.proj_out(kd, 1, T, t["win_ev"], 48, t["proj"], c0)


def emit_C(k, t):
    P = k.P
    R, use = rl_setup(k, t, [0, 1])
    Bx4 = Buf("x4S")
    poolS = t["poolS"]
    zt = k.sb("zt", [128, 8]); Bz = Buf("zt")
    P.op("gpsimd", _m("memset", zt[:], 0.0), writes=[Bz])
    for g in range(4):
        P.dma("sync", poolS[g][0, :, 0:8], zt[:], reads=[Bz], sem=Bz)
        P.dma("sync", poolS[g][1, :, NLAT + 8:NLAT + 16], zt[:], reads=[Bz], sem=Bz)

    def route_for(c0, T):
        def route(n):
            if n >= 4:
                return [(t["kvS"][(n - 4) * 128:(n - 3) * 128, c0:c0 + T], slice(0, T))]
            if c0 < LC:
                return []
            a = c0 - LC
            h = a // NLAT
            loc = a - h * NLAT
            outs = [(poolS[n][h, :, 8 + loc:8 + loc + T], slice(0, T))]
            if h == 1 and loc == 0:
                outs.append((poolS[n][0, :, 8 + NLAT:16 + NLAT], slice(0, 8)))
            if h == 0 and loc + T == NLAT:
                outs.append((poolS[n][1, :, 0:8], slice(T - 8, T)))
            return outs
        return route

    for ti, (c0, T, kd) in enumerate(all_tiles()):
        R.load_x(t["x1S"], c0, T)
        R.load_h(t["mixA"], c0, T)
        use(0)
        R.mixer_out_sublayer(kd, 1, T, t["wout_ev"])
        R.ffn_sublayer(kd, 2, T, t["wgu"][0][1], t["wdn"][0][1])
        use(1)
        R.ffn_sublayer(kd, 0, T, t["wgu"][1][0], t["wdn"][1][0])
        if c0 >= LC:
            a = c0 - LC
            R.store_x(t["x4S"][(a % NLAT) // TT][a // NLAT], 0, T, Bdst=Bx4)
        R.proj_out(kd, 1, T, t["win_pkv"], 12, None, c0, route=route_for(c0, T))
    use(1)
    for j in range(NLAT // TT):
        R.load_x(t["x4S"], 0, TT, dyn=j, Bsrc=Bx4)
        R.proj_out(0, 1, TT, t["win_q"], 12, t["qS"], j * TT)


def emit_E(k, t):
    R, use = rl_setup(k, t, [1]); use(1)
    for j in range(NLAT // TT):
        R.load_x(t["x4S"], 0, TT, dyn=j)
        R.load_h(t["mixB"], j * TT, TT)
        R.mixer_out_sublayer(0, 1, TT, t["wout_od"])
        R.ffn_sublayer(0, 2, TT, t["wgu"][1][1], t["wdn"][1][1])
        R.store_x(t["outT"], j * TT, TT)


def build_fused():
    nc = bass.Bass("TRN2", target_bir_lowering=False)

    def din(name, shape, dt=F32):
        return nc.dram_tensor(name, list(shape), dt, kind="ExternalInput").ap()

    def scr(name, shape, dt=F32):
        return nc.dram_tensor(name, list(shape), dt).ap()

    t = {}
    t["xT"] = din("xT", [D, NTOT]); t["cT"] = din("cT", [128, 16, 8]); t["mw"] = din("mw", [288, 128, 16, 128]); t["mb"] = din("mb", [128, 288])
    t["npre"] = din("npre", [2, 128, 3, 16]); t["npost"] = din("npost", [2, 128, 3, 16])
    t["wgu"] = [[din("wgu%d%d" % (l, j), [2 * NFC, 128, NDC, 128]) for j in range(2)] for l in range(2)]
    t["wdn"] = [[din("wdn%d%d" % (l, j), [NDC, 128, NFC, 128]) for j in range(2)] for l in range(2)]
    t["win_ev"] = din("win_ev", [48, 128, NDC, 128]); t["wout_ev"] = din("wout_ev", [NDC, 128, NDC, 128])
    t["win_pkv"] = din("win_pkv", [12, 128, NDC, 128]); t["win_q"] = din("win_q", [12, 128, NDC, 128]); t["wout_od"] = din("wout_od", [NDC, 128, NDC, 128])
    t["cw"] = din("cw", [128, 8, 5]); t["wab"] = din("wab", [128, 2, 2, 8, 128]); t["bab"] = din("bab", [128, 2, 2, 8]); t["lam"] = din("lam", [128, 2, 8])
    t["cos1T"] = din("cos1T", [128, S_]); t["sin1T"] = din("sin1T", [128, S_])
    t["dmat"] = din("dmat", [128, 4, 128]); t["iq"] = din("iq", [128, 2, 128]); t["jk"] = din("jk", [128, 2]); t["ident"] = din("ident", [128, 128])
    t["dlb"] = din("dlb", [128, 2, 4]); t["gng"] = din("gng", [128, 4, 2])
    t["cosF"] = din("cosF", [128, S_]); t["sinS"] = din("sinS", [128, S_]); t["pm"] = din("pm", [128, 128])
    t["cosQ"] = din("cosQ", [128, NLAT]); t["sinQ"] = din("sinQ", [128, NLAT])
    t["icnt"] = din("icnt", [128, 4, NLAT]); t["hmask"] = din("hmask", [128, 16])
    t["pw"] = din("pw", [128, 4, 128]); t["pscale"] = din("pscale", [128, 4]); t["qkg"] = din("qkg", [128, 2])
    t["outT"] = nc.dram_tensor("outT", [D, NLAT], F32, kind="ExternalOutput").ap()
    t["modS"] = scr("modS", [128, 288, 8]); t["x1S"] = scr("x1S", [D, NTOT]); t["proj"] = scr("proj", [6144, NTOT])
    t["h0s"] = scr("h0s", [1024, NTOT]); t["o0s"] = scr("o0s", [512, NTOT]); t["mixA"] = scr("mixA", [D, NTOT], BF16)
    t["x4S"] = [scr("x4S%d" % j, [2, D, TT]) for j in range(NLAT // TT)]; t["poolS"] = [scr("poolS%d" % g, [2, 128, NLAT + 16]) for g in range(4)]; t["kvS"] = scr("kvS", [1024, NTOT])
    t["qS"] = scr("qS", [1536, NLAT]); t["mixB"] = scr("mixB", [D, NLAT], BF16)
    for nm in ("proj", "mixA", "poolS", "kvS", "qS", "mixB"):
        t["B" + nm] = Buf(nm)

    with phase(nc, "m_") as k:
        emit_mod(k, t)
    with phase(nc, "a_") as k:
        emit_A(k, t)
    with phase(nc, "l_") as k:
        for nm in ("proj", "mixA"):
            t["B" + nm] = Buf(nm)
        emit_B1(k, t, 8)
    for hp in range(2):
        with phase(nc, "r%d_" % hp) as k:
            for nm in ("proj", "mixA"):
                t["B" + nm] = Buf(nm)
            emit_B2(k, t, hp)
    with phase(nc, "c_") as k:
        emit_C(k, t)
    with phase(nc, "d_") as k:
        for nm in ("poolS", "kvS", "qS", "mixB"):
            t["B" + nm] = Buf(nm)
        emit_D(k, t)
    with phase(nc, "e_") as k:
        emit_E(k, t)
    return nc


def host_B1_full(lru_conv_w, lru_conv_b, lru_wa, lru_ba, lru_wx, lru_bx, lru_lambda):
    cwv = np.concatenate([lru_conv_w[0], lru_conv_b[0][None]], 0)
    cw = np.ascontiguousarray(cwv.reshape(5, 8, 128).transpose(2, 1, 0))
    wab = np.stack([lru_wa[0], lru_wx[0]], 0)
    wab = np.ascontiguousarray(wab.transpose(3, 0, 1, 2, 4))
    bab = np.stack([lru_ba[0], lru_bx[0]], 0).reshape(2, 2, 8, 128)
    bab = np.ascontiguousarray(bab.transpose(3, 0, 1, 2))
    lam = np.ascontiguousarray(lru_lambda[0].reshape(2, 8, 128).transpose(2, 0, 1))
    return {"cw": cw.astype(np.float32), "wab": wab.astype(np.float32), "bab": bab.astype(np.float32), "lam": lam.astype(np.float32)}


_NC_CACHE = {}


def kernel(x, c, ctx, c_ctx, mod_w, mod_b, norm_pre, norm_post, ffn_gate, ffn_up, ffn_down,
           ev_w_in, ev_w_out, lru_conv_w, lru_conv_b, lru_wa, lru_ba, lru_wx, lru_bx, lru_lambda,
           ret_decay_logit, ret_gn, od_w_in, od_w_out, pool_w, pool_scale, q_norm, k_norm):
    f32 = lambda a: np.asarray(a, dtype=np.float32)
    x, c, ctx, c_ctx, mod_w, mod_b = map(f32, (x, c, ctx, c_ctx, mod_w, mod_b))
    norm_pre, norm_post, ffn_gate, ffn_up, ffn_down = map(f32, (norm_pre, norm_post, ffn_gate, ffn_up, ffn_down))
    ev_w_in, ev_w_out, od_w_in, od_w_out = map(f32, (ev_w_in, ev_w_out, od_w_in, od_w_out))
    lru_conv_w, lru_conv_b, lru_wa, lru_ba, lru_wx, lru_bx, lru_lambda = map(
        f32, (lru_conv_w, lru_conv_b, lru_wa, lru_ba, lru_wx, lru_bx, lru_lambda))
    ret_decay_logit, ret_gn, pool_w, pool_scale, q_norm, k_norm = map(f32, (ret_decay_logit, ret_gn, pool_w, pool_scale, q_norm, k_norm))

    shared = {}
    shared["mw"] = np.ascontiguousarray(np.concatenate([mod_w[l].reshape(16, 128, 144, 128).transpose(2, 1, 0, 3) for l in range(2)], 0))
    shared["mb"] = np.ascontiguousarray(np.concatenate([mod_b[l].reshape(144, 128).T for l in range(2)], 1))
    shared["npre"] = np.stack([vec16(norm_pre[l]) for l in range(2)], 0); shared["npost"] = np.stack([vec16(norm_post[l]) for l in range(2)], 0)
    for l in range(2):
        for j in range(2):
            shared["wgu%d%d" % (l, j)] = tile_gu(ffn_gate[l, j], ffn_up[l, j])
            shared["wdn%d%d" % (l, j)] = tile_w(ffn_down[l, j], NFC, NDC)
    shared["win_ev"] = tile_w(ev_w_in[0], NDC, 48); shared["wout_ev"] = tile_w(ev_w_out[0], NDC, NDC)
    od = od_w_in[0]
    shared["win_pkv"] = tile_w(np.concatenate([od[:, 0:512], od[:, 2048:3072]], 1), NDC, 12)
    shared["win_q"] = tile_w(od[:, 512:2048], NDC, 12)
    shared["wout_od"] = tile_w(od_w_out[0], NDC, NDC)
    shared.update(host_B1_full(lru_conv_w, lru_conv_b, lru_wa, lru_ba, lru_wx, lru_bx, lru_lambda))
    shared["cos1T"], shared["sin1T"] = host_rot1_tables()
    shared.update(host_ret_consts())
    shared["dlb"] = np.ascontiguousarray(np.broadcast_to(ret_decay_logit[0][None], (128, 2, 4))).astype(np.float32)
    shared["gng"] = np.ascontiguousarray(ret_gn[0].reshape(4, 2, 128).transpose(2, 0, 1)).astype(np.float32)
    cosF, sinS, pm = host_rot2_tables()
    shared["cosF"], shared["sinS"], shared["pm"] = cosF, sinS, pm
    shared["pw"] = np.ascontiguousarray(pool_w[0].transpose(1, 0, 2)); shared["pscale"] = np.ascontiguousarray(pool_scale[0].reshape(4, 128).T)
    shared["qkg"] = np.ascontiguousarray(np.stack([q_norm[0], k_norm[0]], 1))

    maps = []
    for core in range(8):
        b, half = core // 2, core % 2
        m = dict(shared)
        m["xT"] = np.ascontiguousarray(np.concatenate([ctx[b], x[b]], 0).T)
        cc = np.zeros((8, D), np.float32); cc[0] = c[b]; cc[1] = c_ctx
        m["cT"] = np.ascontiguousarray(cc.T.reshape(16, 128, 8).transpose(1, 0, 2))
        m["cosQ"] = np.ascontiguousarray(cosF[:, half * NLAT:(half + 1) * NLAT]); m["sinQ"] = np.ascontiguousarray(sinS[:, half * NLAT:(half + 1) * NLAT])
        m["icnt"] = host_pool_icnt(half)
        hm = np.ones((128, 16), np.float32)
        if half == 0:
            hm[:, 0:8] = 0.0
        else:
            hm[:, 8:16] = 0.0
        m["hmask"] = hm
        maps.append(m)
    if "f" not in _NC_CACHE:
        _NC_CACHE["f"] = build_fused()
    res = run_bass_kernel_spmd(_NC_CACHE["f"], maps, core_ids=list(range(8))).results
    out = np.empty((B_, S_, D), np.float32)
    for core in range(8):
        b, half = core // 2, core % 2
        out[b, half * NLAT:(half + 1) * NLAT, :] = res[core]["outT"].T
    return out
```
